# Optimizing a Trainium2 kernel written in Bass

```python
import jax
import jax.numpy as jnp
from jax import lax
import numpy as np

D_MODEL = 2048
BATCH = 16
SEQ = 256
DEPTH = 2
DEC_BATCH = 4
DEC_SEQ = 2048
PAST_LEN = 256

GRID_W = 64
HEAD_DIM = 128
GLA_H = 8
GLA_DK = 64
GLA_DV = 128
GLA_RANK = 16
GLA_TAU = 16.0
GQA_H = 8
GQA_KV = 2
NA_H = 8
NA_WIN_R = 8
NA_WIN_C = 16
DN_H = 8
DN_DK = 128
DN_DV = 128
CONV_W = 3
CHUNK = 64
Q_BLOCK = 128
D_FF = 4 * D_MODEL
ROPE_THETA = 10000.0
ROPE_FREQ = HEAD_DIM // 4
EPS = 1e-6
N_EVEN = (DEPTH + 1) // 2
N_ODD = DEPTH // 2
EVEN_SPLIT = (GLA_H * GLA_DK, GLA_H * GLA_DK, GLA_H * GLA_DV, GLA_H * GLA_DV, 2 * GLA_RANK,
              GQA_H * HEAD_DIM, GQA_KV * HEAD_DIM, GQA_KV * HEAD_DIM)
EVEN_COLS = sum(EVEN_SPLIT)
EVEN_MIX = GLA_H * GLA_DV + GQA_H * HEAD_DIM
DN_QKV = DN_H * (2 * DN_DK + DN_DV)
ODD_SPLIT = (NA_H * HEAD_DIM, NA_H * HEAD_DIM, NA_H * HEAD_DIM, DN_QKV, DN_H * DN_DV, 2 * DN_H, 2 * DN_H)
ODD_COLS = sum(ODD_SPLIT)
ODD_MIX = NA_H * HEAD_DIM + DN_H * DN_DV

kernel_name = 'bidir_hybrid_dit_step'


def rmsnorm(x, g):
    xf = x.astype(jnp.float32)
    y = xf * lax.rsqrt(jnp.mean(xf * xf, axis=-1, keepdims=True) + EPS)
    return (y * g.astype(jnp.float32)).astype(x.dtype)


def l2norm(x):
    xf = x.astype(jnp.float32)
    return (xf * lax.rsqrt(jnp.sum(xf * xf, axis=-1, keepdims=True) + EPS)).astype(x.dtype)


def split_cols(x, sizes):
    return jnp.split(x, np.cumsum(sizes)[:-1].tolist(), axis=-1)


def rev(x):
    return jnp.flip(x, axis=1)


def adaln(cond, w, b):
    m = jax.nn.silu(cond) @ w + b
    return jnp.split(m[:, None, :], 6, axis=-1)


def axial_rope(n_tok):
    t = jnp.arange(n_tok)
    inv = 1.0 / (ROPE_THETA ** (jnp.arange(ROPE_FREQ, dtype=jnp.float32) / ROPE_FREQ))
    pos = jnp.stack([t // GRID_W, t % GRID_W], axis=1).astype(jnp.float32)
    ang = pos[:, :, None] * inv
    return jnp.cos(ang), jnp.sin(ang)


def apply_rope(x, cos, sin):
    b, t, h, d = x.shape
    xr = x.astype(jnp.float32).reshape(b, t, h, 2, 2, ROPE_FREQ)
    x1, x2 = xr[..., 0, :], xr[..., 1, :]
    c, s = cos[None, :, None], sin[None, :, None]
    out = jnp.stack([x1 * c - x2 * s, x2 * c + x1 * s], axis=-2)
    return out.reshape(b, t, h, d).astype(x.dtype)


def block_attention(q, k, v):
    b, t, h, d = q.shape
    kv = k.shape[2]
    qb = q.reshape(b, t // Q_BLOCK, Q_BLOCK, kv, h // kv, d).transpose(1, 0, 2, 3, 4, 5)

    def one(qi):
        s = jnp.einsum('bqhgd,bkhd->bhgqk', qi, k).astype(jnp.float32) * (d ** -0.5)
        p = jax.nn.softmax(s, axis=-1).astype(v.dtype)
        return jnp.einsum('bhgqk,bkhd->bqhgd', p, v)

    o = lax.map(one, qb)
    return o.transpose(1, 0, 2, 3, 4, 5).reshape(b, t, h, d)


def neighbourhood_attention(q, k, v, k_ctx, v_ctx, rpb):
    b, t, h, d = q.shape
    rows = t // GRID_W
    wr = min(NA_WIN_R, rows)
    nwin = wr * NA_WIN_C
    qg = q.reshape(b, rows, GRID_W, h, d)
    kg = k.reshape(b, rows, GRID_W, h, d)
    vg = v.reshape(b, rows, GRID_W, h, d)
    cols = jnp.arange(GRID_W)
    col_idx = jnp.clip(cols - NA_WIN_C // 2, 0, GRID_W - NA_WIN_C)[:, None] + jnp.arange(NA_WIN_C)
    col_bias = rpb[:, :, col_idx - cols[:, None] + NA_WIN_C - 1]

    def one_row(r):
        rs = jnp.clip(r - wr // 2, 0, rows - wr)
        q_r = lax.dynamic_index_in_dim(qg, r, axis=1, keepdims=False)
        k_win = lax.dynamic_slice_in_dim(kg, rs, wr, axis=1)[:, :, col_idx]
        v_win = lax.dynamic_slice_in_dim(vg, rs, wr, axis=1)[:, :, col_idx]
        bias = col_bias[:, rs + jnp.arange(wr) - r + NA_WIN_R - 1]
        s_win = jnp.einsum('bqhd,brqchd->bhqrc', q_r, k_win).astype(jnp.float32) * (d ** -0.5)
        s_win = s_win + jnp.transpose(bias, (0, 2, 1, 3)).astype(jnp.float32)
        s_ctx = jnp.einsum('bqhd,bkhd->bhqk', q_r, k_ctx).astype(jnp.float32) * (d ** -0.5)
        s = jnp.concatenate([s_win.reshape(b, h, GRID_W, nwin), s_ctx], axis=-1)
        p = jax.nn.softmax(s, axis=-1).astype(v.dtype)
        p_win = p[..., :nwin].reshape(b, h, GRID_W, wr, NA_WIN_C)
        return (jnp.einsum('bhqrc,brqchd->bqhd', p_win, v_win)
                + jnp.einsum('bhqk,bkhd->bqhd', p[..., nwin:], v_ctx))

    o = lax.map(one_row, jnp.arange(rows))
    return o.transpose(1, 0, 2, 3, 4).reshape(b, t, h, d)


def to_chunks(x):
    b, t, h, d = x.shape
    return x.astype(jnp.float32).reshape(b, t // CHUNK, CHUNK, h, d).transpose(1, 0, 3, 2, 4)


def from_chunks(o):
    n, b, h, c, d = o.shape
    return o.transpose(1, 0, 3, 2, 4).reshape(b, n * c, h, d)


def gla_chunked(q, k, v, log_a, s0):
    qc, kc, vc, ac = to_chunks(q), to_chunks(k), to_chunks(v), to_chunks(log_a)
    bcum = jnp.cumsum(ac, axis=3)
    b_last = bcum[:, :, :, -1:, :]
    qe = qc * jnp.exp(bcum)
    ke = kc * jnp.exp(-bcum)
    kd = kc * jnp.exp(b_last - bcum)
    tril = jnp.tril(jnp.ones((CHUNK, CHUNK), dtype=bool))
    scores = jnp.where(tril, jnp.einsum('nbhcd,nbhsd->nbhcs', qe, ke), 0.0)
    o_intra = jnp.einsum('nbhcs,nbhsv->nbhcv', scores, vc)
    dec = jnp.exp(b_last[:, :, :, 0, :])[..., None]

    def step(S, inp):
        qe_i, kd_i, v_i, dec_i = inp
        o_i = jnp.einsum('bhcd,bhdv->bhcv', qe_i, S)
        S = S * dec_i + jnp.einsum('bhcd,bhcv->bhdv', kd_i, v_i)
        return S, o_i

    S, o_inter = lax.scan(step, s0.astype(jnp.float32), (qe, kd, vc, dec))
    return from_chunks(o_inter + o_intra).astype(v.dtype), S


def delta_chunked(q, k, v, beta, g, s0):
    qc, kc, vc = to_chunks(q), to_chunks(k), to_chunks(v)
    bc = to_chunks(beta[..., None])[..., 0]
    gcum = jnp.cumsum(to_chunks(g[..., None])[..., 0], axis=-1)
    tril = jnp.tril(jnp.ones((CHUNK, CHUNK), dtype=bool))
    strict = jnp.tril(jnp.ones((CHUNK, CHUNK), dtype=bool), -1)
    diff = gcum[..., :, None] - gcum[..., None, :]
    L = jnp.where(tril, jnp.exp(jnp.where(tril, diff, 0.0)), 0.0)
    kb = kc * bc[..., None]
    M = jnp.where(strict, jnp.einsum('nbhcd,nbhsd->nbhcs', kb, kc) * L, 0.0)
    A = M + jnp.eye(CHUNK, dtype=jnp.float32)
    rhs = jnp.concatenate([vc * bc[..., None], kb * jnp.exp(gcum)[..., None]], axis=-1)
    sol = lax.linalg.triangular_solve(A, rhs, left_side=True, lower=True, unit_diagonal=True)
    dv = vc.shape[-1]
    u0, kcum = sol[..., :dv], sol[..., dv:]
    attn = jnp.einsum('nbhcd,nbhsd->nbhcs', qc, kc) * L
    qe = qc * jnp.exp(gcum)[..., None]
    g_last = gcum[..., -1:]
    kd = kc * jnp.exp(g_last - gcum)[..., None]
    dec = jnp.exp(g_last)[..., None]

    def step(S, inp):
        u0_i, kcum_i, qe_i, attn_i, kd_i, dec_i = inp
        u = u0_i - jnp.einsum('bhcd,bhdv->bhcv', kcum_i, S)
        o_i = jnp.einsum('bhcd,bhdv->bhcv', qe_i, S) + jnp.einsum('bhcs,bhsv->bhcv', attn_i, u)
        S = S * dec_i + jnp.einsum('bhcd,bhcv->bhdv', kd_i, u)
        return S, o_i

    S, o = lax.scan(step, s0.astype(jnp.float32), (u0, kcum, qe, attn, kd, dec))
    return from_chunks(o).astype(v.dtype), S


def centred_conv(x, w):
    pad = CONV_W // 2
    t = x.shape[1]
    xp = jnp.pad(x, ((0, 0), (pad, pad), (0, 0)))
    out = xp[:, 0:t] * w[0]
    for j in range(1, CONV_W):
        out = out + xp[:, j:j + t] * w[j]
    return out


def even_mixer(h, w_in, w_a2, b_a2, gla_norm, q_norm, k_norm, w_out, ctx, rope):
    b, t, _ = h.shape
    gq, gk, gv, gg, glo, aq, ak, av = split_cols(h @ w_in, EVEN_SPLIT)
    gq = gq.reshape(b, t, GLA_H, GLA_DK) * (GLA_DK ** -0.5)
    gk = gk.reshape(b, t, GLA_H, GLA_DK)
    gv = gv.reshape(b, t, GLA_H, GLA_DV)
    la = jnp.einsum('btdr,drk->btdk', glo.reshape(b, t, 2, GLA_RANK), w_a2) + b_a2
    la = (jax.nn.log_sigmoid(la.astype(jnp.float32)) / GLA_TAU).reshape(b, t, 2, GLA_H, GLA_DK)
    if ctx is None:
        s0 = jnp.zeros((b, 2, GLA_H, GLA_DK, GLA_DV), jnp.float32)
    else:
        s0 = ctx[0]
    o_f, s_f = gla_chunked(gq, gk, gv, la[:, :, 0], s0[:, 0])
    o_b, s_b = gla_chunked(rev(gq), rev(gk), rev(gv), rev(la[:, :, 1]), s0[:, 1])
    o_gla = rmsnorm(o_f + rev(o_b), gla_norm) * jax.nn.silu(gg.reshape(b, t, GLA_H, GLA_DV))

    aq = rmsnorm(aq.reshape(b, t, GQA_H, HEAD_DIM), q_norm)
    ak = rmsnorm(ak.reshape(b, t, GQA_KV, HEAD_DIM), k_norm)
    av = av.reshape(b, t, GQA_KV, HEAD_DIM)
    if ctx is None:
        o_att = block_attention(aq, ak, av)
        new_ctx = (jnp.stack([s_f, s_b], axis=1), ak, av)
    else:
        qr = apply_rope(aq, rope[0], rope[1])
        kr = apply_rope(ak, rope[0], rope[1])
        o_att = block_attention(qr, jnp.concatenate([ctx[1], kr], axis=1), jnp.concatenate([ctx[2], av], axis=1))
        new_ctx = None
    mix = jnp.concatenate([o_gla.reshape(b, t, -1), o_att.reshape(b, t, -1)], axis=-1)
    return mix @ w_out, new_ctx


def odd_mixer(h, w_in, conv_w, a_log, dt_bias, dn_norm, rpb, w_out, ctx):
    b, t, _ = h.shape
    nq, nk, nv, dqkv, dz, da, db = split_cols(h @ w_in, ODD_SPLIT)
    nq = nq.reshape(b, t, NA_H, HEAD_DIM)
    nk = nk.reshape(b, t, NA_H, HEAD_DIM)
    nv = nv.reshape(b, t, NA_H, HEAD_DIM)
    if ctx is None:
        o_na = block_attention(nq, nk, nv)
        s0 = jnp.zeros((b, 2, DN_H, DN_DK, DN_DV), jnp.float32)
    else:
        o_na = neighbourhood_attention(nq, nk, nv, ctx[1], ctx[2], rpb)
        s0 = ctx[0]
    dqkv = jax.nn.silu(centred_conv(dqkv, conv_w))
    dq, dk, dv = split_cols(dqkv, (DN_H * DN_DK, DN_H * DN_DK, DN_H * DN_DV))
    dq = l2norm(dq.reshape(b, t, DN_H, DN_DK)) * (DN_DK ** -0.5)
    dk = l2norm(dk.reshape(b, t, DN_H, DN_DK))
    dv = dv.reshape(b, t, DN_H, DN_DV)
    beta = jax.nn.sigmoid(db.reshape(b, t, 2, DN_H).astype(jnp.float32))
    g = -jnp.exp(a_log.astype(jnp.float32)) * jax.nn.softplus(
        da.reshape(b, t, 2, DN_H).astype(jnp.float32) + dt_bias.astype(jnp.float32))
    o_f, s_f = delta_chunked(dq, dk, dv, beta[:, :, 0], g[:, :, 0], s0[:, 0])
    o_b, s_b = delta_chunked(rev(dq), rev(dk), rev(dv), rev(beta[:, :, 1]), rev(g[:, :, 1]), s0[:, 1])
    o_dn = rmsnorm(o_f + rev(o_b), dn_norm) * jax.nn.silu(dz.reshape(b, t, DN_H, DN_DV))
    new_ctx = (jnp.stack([s_f, s_b], axis=1), nk, nv) if ctx is None else None
    mix = jnp.concatenate([o_na.reshape(b, t, -1), o_dn.reshape(b, t, -1)], axis=-1)
    return mix @ w_out, new_ctx


def setup_inputs(seed: int = 0) -> dict:
    key = jax.random.key(seed)
    ks = iter(jax.random.split(key, 40))

    def nrm(shape, scale):
        return jax.random.normal(next(ks), shape, jnp.float32) * scale

    D = D_MODEL
    a_log = jnp.log(jax.random.uniform(next(ks), (N_ODD, 2, DN_H), jnp.float32, 1.0, 16.0))
    dt = jnp.exp(jax.random.uniform(next(ks), (N_ODD, 2, DN_H), jnp.float32,
                                    float(np.log(1e-3)), float(np.log(1e-1))))
    dt_bias = dt + jnp.log(-jnp.expm1(-dt))
    return {
        'x_prompt': nrm((BATCH, SEQ, D), 1.0),
        'x_sample': nrm((DEC_BATCH, DEC_SEQ, D), 1.0),
        'state_gla': nrm((DEC_BATCH, N_EVEN, 2, GLA_H, GLA_DK, GLA_DV), 0.5),
        'cache_gqa_k': nrm((DEC_BATCH, N_EVEN, PAST_LEN, GQA_KV, HEAD_DIM), 1.0),
        'cache_gqa_v': nrm((DEC_BATCH, N_EVEN, PAST_LEN, GQA_KV, HEAD_DIM), 1.0),
        'cache_na_k': nrm((DEC_BATCH, N_ODD, PAST_LEN, NA_H, HEAD_DIM), 1.0),
        'cache_na_v': nrm((DEC_BATCH, N_ODD, PAST_LEN, NA_H, HEAD_DIM), 1.0),
        'state_delta': nrm((DEC_BATCH, N_ODD, 2, DN_H, DN_DK, DN_DV), 0.5),
        'c': nrm((DEC_BATCH, D), 1.0),
        'c_ctx': nrm((D,), 1.0),
        'norm1': 1.0 + nrm((DEPTH, D), 0.1),
        'norm2': 1.0 + nrm((DEPTH, D), 0.1),
        'w_ada': nrm((DEPTH, D, 6 * D), 0.5 * D ** -0.5),
        'b_ada': nrm((DEPTH, 6 * D), 0.01),
        'w_mlp1': nrm((DEPTH, D, D_FF), D ** -0.5),
        'w_mlp2': nrm((DEPTH, D_FF, D), D_FF ** -0.5),
        'ev_w_in': nrm((N_EVEN, D, EVEN_COLS), D ** -0.5),
        'ev_w_a2': nrm((N_EVEN, 2, GLA_RANK, GLA_H * GLA_DK), GLA_RANK ** -0.5),
        'ev_b_a2': nrm((N_EVEN, 2, GLA_H * GLA_DK), 0.1),
        'ev_gla_norm': 1.0 + nrm((N_EVEN, GLA_DV), 0.1),
        'ev_q_norm': 1.0 + nrm((N_EVEN, HEAD_DIM), 0.1),
        'ev_k_norm': 1.0 + nrm((N_EVEN, HEAD_DIM), 0.1),
        'ev_w_out': nrm((N_EVEN, EVEN_MIX, D), EVEN_MIX ** -0.5),
        'od_w_in': nrm((N_ODD, D, ODD_COLS), D ** -0.5),
        'od_conv': nrm((N_ODD, CONV_W, DN_QKV), CONV_W ** -0.5),
        'od_a_log': a_log,
        'od_dt_bias': dt_bias,
        'od_dn_norm': 1.0 + nrm((N_ODD, DN_DV), 0.1),
        'od_rpb': nrm((N_ODD, NA_H, 2 * NA_WIN_R - 1, 2 * NA_WIN_C - 1), 0.1),
        'od_w_out': nrm((N_ODD, ODD_MIX, D), ODD_MIX ** -0.5),
        'norm_f': 1.0 + nrm((D,), 0.1),
    }


def reference(x_prompt, x_sample, state_gla, cache_gqa_k, cache_gqa_v, cache_na_k, cache_na_v, state_delta,
              c, c_ctx, norm1, norm2, w_ada, b_ada, w_mlp1, w_mlp2,
              ev_w_in, ev_w_a2, ev_b_a2, ev_gla_norm, ev_q_norm, ev_k_norm, ev_w_out,
              od_w_in, od_conv, od_a_log, od_dt_bias, od_dn_norm, od_rpb, od_w_out, norm_f):

    def trunk(x, cond, caches, rope):
        new = []
        for i in range(DEPTH):
            j = i // 2
            sh1, sc1, gt1, sh2, sc2, gt2 = adaln(cond, w_ada[i], b_ada[i])
            h = rmsnorm(x, norm1[i]) * (1.0 + sc1) + sh1
            ctx = None if caches is None else caches[i]
            if i % 2 == 0:
                y, nc = even_mixer(h, ev_w_in[j], ev_w_a2[j], ev_b_a2[j], ev_gla_norm[j], ev_q_norm[j],
                                   ev_k_norm[j], ev_w_out[j], ctx, rope)
            else:
                y, nc = odd_mixer(h, od_w_in[j], od_conv[j], od_a_log[j], od_dt_bias[j], od_dn_norm[j],
                                  od_rpb[j], od_w_out[j], ctx)
            new.append(nc)
            x = x + gt1 * y
            h = rmsnorm(x, norm2[i]) * (1.0 + sc2) + sh2
            x = x + gt2 * (jnp.square(jax.nn.relu(h @ w_mlp1[i])) @ w_mlp2[i])
        return rmsnorm(x, norm_f), new

    y_prompt, new_p = trunk(x_prompt, c_ctx[None, :], None, None)
    st_gla = jnp.stack([new_p[i][0] for i in range(0, DEPTH, 2)], axis=1)
    ck_gqa = jnp.stack([new_p[i][1] for i in range(0, DEPTH, 2)], axis=1)
    cv_gqa = jnp.stack([new_p[i][2] for i in range(0, DEPTH, 2)], axis=1)
    ck_na = jnp.stack([new_p[i][1] for i in range(1, DEPTH, 2)], axis=1)
    cv_na = jnp.stack([new_p[i][2] for i in range(1, DEPTH, 2)], axis=1)
    st_dn = jnp.stack([new_p[i][0] for i in range(1, DEPTH, 2)], axis=1)

    caches = [(state_gla[:, i // 2], cache_gqa_k[:, i // 2], cache_gqa_v[:, i // 2]) if i % 2 == 0
              else (state_delta[:, i // 2], cache_na_k[:, i // 2], cache_na_v[:, i // 2])
              for i in range(DEPTH)]
    y_sample, _ = trunk(x_sample, c, caches, axial_rope(x_sample.shape[1]))

    return (y_prompt, y_sample, st_gla, ck_gqa, cv_gqa, ck_na, cv_na, st_dn)
```

```python
import contextlib
import numpy as np
import concourse.bass as bass
import concourse.mybir as mybir
from concourse.bass_utils import run_bass_kernel_spmd

F32 = mybir.dt.float32
BF16 = mybir.dt.bfloat16
AF = mybir.ActivationFunctionType
ALU = mybir.AluOpType
AX = mybir.AxisListType
PE, ACT, DVE, POOL, SP = "tensor", "scalar", "vector", "gpsimd", "sync"

NCORES = 8
D = 2048
DFF = 8192
NP_TOK = 512
NS_TOK = 2048
PAST = 256
EV_COLS = 4640
OD_COLS = 7200
EPS = 1e-6
DEBUG = False
UPTO = 99


class Trk:
    __slots__ = ("last_w", "readers")

    def __init__(self):
        self.last_w = None
        self.readers = []


class Op:
    __slots__ = ("eng", "fn", "deps", "signal", "sig", "is_dma", "lane")

    def __init__(self, eng, fn, is_dma, lane):
        self.eng = eng
        self.fn = fn
        self.deps = []
        self.signal = False
        self.sig = None
        self.is_dma = is_dma
        self.lane = lane


class Prog:
    def __init__(self, nc):
        self.nc = nc
        self.ops = []
        self.stack = contextlib.ExitStack()
        self.lane_last = {}
        self.trks = []

    def trk(self):
        t = Trk()
        self.trks.append(t)
        return t

    def op(self, eng, fn, R=(), W=(), is_dma=False, lane=None):
        o = Op(eng, fn, is_dma, lane)
        deps = []
        for r in R:
            if r.last_w is not None:
                deps.append(r.last_w)
        for w in W:
            if w.last_w is not None:
                deps.append(w.last_w)
            deps.extend(w.readers)
        if is_dma:
            prev = self.lane_last.get(lane)
            if prev is not None:
                deps.append(prev)
            self.lane_last[lane] = o
        seen = set()
        for d in deps:
            if d is o or id(d) in seen:
                continue
            seen.add(id(d))
            if (not d.is_dma) and (not is_dma) and d.eng == PE and eng == PE:
                continue
            o.deps.append(d)
            d.signal = True
        for r in R:
            r.readers.append(o)
        for w in W:
            w.last_w = o
            w.readers = []
        self.ops.append(o)
        return o

    def barrier(self):
        allt = list(self.trks)
        deps = []
        seen = set()
        for t in allt:
            for d in ([t.last_w] if t.last_w is not None else []) + t.readers:
                if id(d) not in seen:
                    seen.add(id(d))
                    deps.append(d)
        for l, d in self.lane_last.items():
            if id(d) not in seen:
                seen.add(id(d))
                deps.append(d)
        for e in (PE, ACT, DVE, POOL, SP):
            o = Op(e, None, False, None)
            for d in deps:
                if d.fn is None:
                    continue
                o.deps.append(d)
                d.signal = True
            self.ops.append(o)
        for t in allt:
            t.last_w = None
            t.readers = []

    def dma(self, q, out, in_, R, W, lane):
        return self.op(q, lambda e: e.dma_start(out=out, in_=in_), R, W, True, lane)

    def mm(self, out, lhsT, rhs, start, stop, R, W):
        return self.op(PE, lambda e: e.matmul(out, lhsT=lhsT, rhs=rhs, start=start, stop=stop), R, W)

    def tr(self, out, in_, ident, R, W):
        return self.op(PE, lambda e: e.transpose(out=out, in_=in_, identity=ident), R, W)

    def act(self, out, in_, func, R, W, **kw):
        return self.op(ACT, lambda e: e.activation(out=out, in_=in_, func=func, **kw), R, W)

    def tt(self, eng, out, in0, in1, op, R, W):
        return self.op(eng, lambda e: e.tensor_tensor(out=out, in0=in0, in1=in1, op=op), R, W)

    def ts(self, eng, out, in0, s1, s2, op0, op1, R, W):
        if s2 is None:
            return self.op(eng, lambda e: e.tensor_scalar(out=out, in0=in0, scalar1=s1, scalar2=None, op0=op0), R, W)
        return self.op(eng, lambda e: e.tensor_scalar(out=out, in0=in0, scalar1=s1, scalar2=s2, op0=op0, op1=op1), R, W)

    def stt(self, eng, out, in0, scalar, in1, op0, op1, R, W):
        return self.op(eng, lambda e: e.scalar_tensor_tensor(out=out, in0=in0, scalar=scalar, in1=in1, op0=op0, op1=op1), R, W)

    def cp(self, eng, out, in_, R, W):
        if eng == ACT:
            return self.op(ACT, lambda e: e.copy(out=out, in_=in_), R, W)
        return self.op(eng, lambda e: e.tensor_copy(out=out, in_=in_), R, W)

    def red(self, out, in_, op, R, W):
        return self.op(DVE, lambda e: e.tensor_reduce(out=out, in_=in_, axis=AX.X, op=op), R, W)

    def memset(self, eng, ap, val, W):
        return self.op(eng, lambda e: e.memset(ap, val), (), W)

    def recip(self, out, in_, R, W):
        return self.op(DVE, lambda e: e.reciprocal(out=out, in_=in_), R, W)

    def emit(self):
        nc = self.nc
        engs = [PE, ACT, DVE, POOL, SP]
        esem = {e: self.stack.enter_context(nc.semaphore(f"s_{e}")) for e in engs}
        lanes = {}
        for o in self.ops:
            if o.is_dma and o.lane not in lanes:
                lanes[o.lane] = self.stack.enter_context(nc.semaphore(f"l_{len(lanes)}"))
        cnt = {e: 0 for e in engs}
        lcnt = {l: 0 for l in lanes}
        for o in self.ops:
            if o.fn is None:
                continue
            if o.is_dma:
                lcnt[o.lane] += 16
                o.sig = (lanes[o.lane], lcnt[o.lane])
                o.signal = True
            elif o.signal:
                cnt[o.eng] += 1
                o.sig = (esem[o.eng], cnt[o.eng])
        per = {e: [] for e in engs}
        for o in self.ops:
            per[o.eng].append(o)
        final_waits = [(lanes[l], lcnt[l]) for l in lanes]
        with nc.Block() as block:
            def body(ename):
                def run(eng):
                    waited = {}
                    for o in per[ename]:
                        need = {}
                        for d in o.deps:
                            if d.sig is None:
                                continue
                            s, v = d.sig
                            k = id(s)
                            if waited.get(k, 0) >= v:
                                continue
                            if k not in need or need[k][1] < v:
                                need[k] = (s, v)
                        for k, (s, v) in need.items():
                            eng.wait_ge(s, v)
                            waited[k] = v
                        if o.fn is None:
                            continue
                        ins = o.fn(eng)
                        if o.signal:
                            ins.then_inc(o.sig[0], 16 if o.is_dma else 1)
                    if ename == SP:
                        for s, v in final_waits:
                            eng.wait_ge(s, v)
                return run
            block.tensor(body(PE))
            block.scalar(body(ACT))
            block.vector(body(DVE))
            block.gpsimd(body(POOL))
            block.sync(body(SP))
        self.stack.close()


class Tile:
    __slots__ = ("ap", "t")

    def __init__(self, ap, t):
        self.ap = ap
        self.t = t


class Arena:
    def __init__(self, P, base_ap, nwords):
        self.P = P
        self.base = base_ap
        self.n = nwords
        self.off = 0
        self.static_off = 0

    def reset(self, mixer=False):
        self.off = self.mixer_off if mixer else self.static_off

    def freeze(self):
        self.static_off = self.off

    def f32(self, nw, shape=None):
        assert self.off + nw <= self.n, f"arena overflow {self.off}+{nw}>{self.n}"
        ap = self.base[:, self.off:self.off + nw]
        self.off += nw
        if shape:
            ap = _shape(ap, shape)
        return Tile(ap, self.P.trk())

    def bf16(self, n, shape=None):
        nw = (n + 1) // 2
        assert self.off + nw <= self.n, f"arena overflow {self.off}+{nw}>{self.n}"
        ap = self.base[:, self.off:self.off + nw].bitcast(BF16)
        self.off += nw
        if shape:
            ap = _shape(ap, shape)
        return Tile(ap, self.P.trk())


def _shape(ap, shape):
    if len(shape) == 2:
        return ap.rearrange("p (a b) -> p a b", a=shape[0], b=shape[1])
    if len(shape) == 3:
        return ap.rearrange("p (a b c) -> p a b c", a=shape[0], b=shape[1], c=shape[2])
    if len(shape) == 4:
        return ap.rearrange("p (a b c d) -> p a b c d", a=shape[0], b=shape[1], c=shape[2], d=shape[3])
    raise ValueError


def bc_mid(ap2, n):
    p, f = ap2.shape
    return ap2.unsqueeze(1).to_broadcast([p, n, f])


def bc_last(ap2, n):
    p, h = ap2.shape
    return ap2.unsqueeze(2).to_broadcast([p, h, n])


class K:
    pass


def dram_in(nc, name, shape, dt=F32):
    return nc.dram_tensor(name, list(shape), dt, kind="ExternalInput").ap()


def dram_out(nc, name, shape, dt=F32):
    return nc.dram_tensor(name, list(shape), dt, kind="ExternalOutput").ap()


def dram_scr(nc, name, shape, dt=F32):
    kind = "ExternalOutput" if DEBUG else "Internal"
    return nc.dram_tensor(name, list(shape), dt, kind=kind).ap()


INPUT_SHAPES = {
    "xp": (NP_TOK, D), "xs": (NS_TOK, D), "condT": (128, 32),
    "sgla": (2, 8, 64, 128), "cgk": (PAST, 256), "cgv": (PAST, 256), "cnk": (PAST, 1024), "cnv": (PAST, 1024),
    "sdl": (2, 8, 128, 128),
    "w_ada": (2, D, 6 * D), "b_ada": (2, 6 * D), "norm1": (2, D), "norm2": (2, D),
    "w_mlp1": (2, D, DFF), "w_mlp2": (2, DFF, D),
    "ev_w_in": (D, EV_COLS), "wa2": (2, 17, 512), "gla_norm": (1, 128), "qk_gain": (1, 1280),
    "ev_w_out": (D, D), "od_w_in": (D, OD_COLS), "od_conv": (3, 3072), "od_alog": (1, 16), "od_dtb": (1, 16),
    "dn_norm": (1, 128), "na_bias": (5, 128, 8 * 576), "na_mask": (5, 128, 576), "od_w_out": (D, D), "norm_f": (1, D),
    "ident": (128, 128), "tri": (11, 128, 128), "rope": (NS_TOK, 128),
}


def build_program():
    nc = bass.Bass("TRN2", target_bir_lowering=False)
    k = K()
    k.nc = nc
    k.i = {n: dram_in(nc, n, s) for n, s in INPUT_SHAPES.items()}
    k.o = {
        "yp": dram_out(nc, "yp", (NP_TOK, D)), "ys": dram_out(nc, "ys", (NS_TOK, D)),
        "nsg": dram_out(nc, "nsg", (2, 2, 8, 64, 128)),
        "ngk": dram_out(nc, "ngk", (NP_TOK, 256)), "ngv": dram_out(nc, "ngv", (NP_TOK, 256)),
        "nnk": dram_out(nc, "nnk", (NP_TOK, 1024)), "nnv": dram_out(nc, "nnv", (NP_TOK, 1024)),
        "nsd": dram_out(nc, "nsd", (2, 2, 8, 128, 128)),
    }
    k.mods = dram_scr(nc, "mods", (2, 2, 6 * D))
    k.wc_dram = nc.dram_tensor("wcache", [110, 128, 8192], BF16, kind="Internal").ap()
    k.wcache = {}
    P = Prog(nc)
    k.P = P
    sb = P.stack.enter_context(nc.sbuf_tensor("arena", [128, 52000], F32))
    k.ar = Arena(P, sb, 52000)
    ps = P.stack.enter_context(nc.psum_tensor("psum", [128, 4096], F32))
    k.ps = ps
    k.bank_t = [P.trk() for _ in range(8)]

    class G:
        pass
    gp, gs = G(), G()
    gp.name, gp.NT, gp.x_in, gp.row, gp.y = "p", NP_TOK, k.i["xp"], 0, k.o["yp"]
    gs.name, gs.NT, gs.x_in, gs.row, gs.y = "s", NS_TOK, k.i["xs"], 1, k.o["ys"]
    gp.seqs = [(0, 256, False, 0), (256, 256, False, 1)]
    gs.seqs = [(0, 2048, True, 0)]
    for g in (gp, gs):
        g.xres = dram_scr(nc, f"xres_{g.name}", (g.NT, D))
        g.zin = dram_scr(nc, f"zin_{g.name}", (g.NT, OD_COLS))
        g.nqkT = dram_scr(nc, f"nqkT_{g.name}", (2048, g.NT), BF16)
        g.mix = dram_scr(nc, f"mix_{g.name}", (g.NT, D), BF16)
        g.t_xres = [P.trk() for _ in range(g.NT // 512)]
        g.t_zin = P.trk()
        g.t_nqkT = P.trk()
        g.t_mix = P.trk()
        g.ofw = dram_scr(nc, f"ofw_{g.name}", (g.NT, 1024))
        g.t_ofw = P.trk()
        g.obw = dram_scr(nc, f"obw_{g.name}", (g.NT, 1024))
        g.t_obw = P.trk()
    k.groups = [gp, gs]
    k.t_mods = P.trk()

    setup_static(k)
    if UPTO >= 1:
        phase_ada(k)
    for l in range(2):
        if UPTO >= 2 + 4 * l:
            for g in k.groups:
                phase_A(k, l, g)
        if UPTO >= 3 + 4 * l:
            for g in k.groups:
                for sq in g.seqs:
                    if l == 0:
                        mixer_even(k, g, sq)
                    else:
                        mixer_odd(k, g, sq)
        if UPTO >= 4 + 4 * l:
            for g in k.groups:
                phase_C(k, l, g)
    P.emit()
    return nc


def bank(k, b, nb=1):
    return k.ps[:, b * 512:(b + nb) * 512]


def bank_bf(k, b):
    return k.ps[:, b * 512:(b + 1) * 512].bitcast(BF16)


def setup_static(k):
    P, ar = k.P, k.ar
    k.identf = ar.f32(128)
    k.identb = ar.bf16(128)
    k.ones_f = ar.f32(128)
    k.small = ar.f32(64)
    ar.mixer_off = ar.off
    k.wring = [ar.bf16(16 * 512, (16, 512)) for _ in range(3)]
    k.wr_i = 0
    P.dma(SP, k.identf.ap, k.i["ident"], [], [k.identf.t], "c0")
    P.cp(DVE, k.identb.ap, k.identf.ap, [k.identf.t], [k.identb.t])
    P.memset(DVE, k.ones_f.ap, 1.0, [k.ones_f.t])
    ar.freeze()


def wload(k, src, key=None):
    P = k.P
    i = k.wr_i
    k.wr_i = (i + 1) % 3
    wt = k.wring[i]
    n = src.shape[1]
    ent = k.wcache.get(key) if key is not None else None
    if ent is not None:
        idx, tk = ent
        P.dma(POOL, wt.ap[:, :, 0:n], k.wc_dram[idx, :, 0:16 * n].rearrange("p (c n) -> p c n", n=n), [tk], [wt.t], f"w{i}")
        return wt
    P.dma(POOL, wt.ap[:, :, 0:n], src.rearrange("(c p) n -> p c n", p=128), [], [wt.t], f"w{i}")
    if key is not None and len(k.wcache) < k.wc_dram.shape[0]:
        idx = len(k.wcache)
        tk = P.trk()
        k.wcache[key] = (idx, tk)
        P.dma(SP, k.wc_dram[idx, :, 0:16 * n].rearrange("p (c n) -> p c n", n=n), wt.ap[:, :, 0:n], [wt.t], [tk], f"wc{i}")
    return wt


def load_bc(k, dst, src_row, R=()):
    k.P.dma(SP, dst.ap, src_row.partition_broadcast(128), list(R), [dst.t], "bc")


def phase_ada(k):
    P, ar = k.P, k.ar
    ar.reset()
    P.barrier()
    cT = ar.f32(32)
    scT = ar.bf16(32, (16, 2))
    brow = [ar.f32(512) for _ in range(2)]
    orow = [ar.f32(512) for _ in range(2)]
    P.dma(SP, cT.ap, k.i["condT"], [], [cT.t], "c0")
    P.act(scT.ap.rearrange("p a b -> p (a b)"), cT.ap, AF.Silu, [cT.t], [scT.t])
    it = 0
    for l in range(2):
        for nb in range(24):
            wt = wload(k, k.i["w_ada"][l, :, nb * 512:(nb + 1) * 512])
            b = it % 4
            pb = bank(k, b)
            for c in range(16):
                P.mm(pb[0:2, :], scT.ap[:, c, :], wt.ap[:, c, :], c == 0, c == 15, [scT.t, wt.t], [k.bank_t[b]])
            br, orw = brow[it % 2], orow[it % 2]
            P.dma(SP, br.ap[0:2, :], k.i["b_ada"][l:l + 1, nb * 512:(nb + 1) * 512].partition_broadcast(2), [], [br.t], f"ab{it % 2}")
            P.tt(DVE, orw.ap[0:2, :], pb[0:2, :], br.ap[0:2, :], ALU.add, [k.bank_t[b], br.t], [orw.t])
            P.dma(SP, k.mods[l, :, nb * 512:(nb + 1) * 512], orw.ap[0:2, :], [orw.t], [k.t_mods], f"ao{it % 2}")
            it += 1


def mod_row(k, l, g, j):
    return k.mods[l, g.row:g.row + 1, j * D:(j + 1) * D]


def make_AB(k, l, g, jsc, jsh, normname, A, B, tmp):
    P = k.P
    load_bc(k, tmp, mod_row(k, l, g, jsc), [k.t_mods])
    load_bc(k, A, k.i[normname][l:l + 1, :])
    P.stt(DVE, A.ap, tmp.ap, 1.0, A.ap, ALU.add, ALU.mult, [tmp.t, A.t], [A.t])
    load_bc(k, B, mod_row(k, l, g, jsh), [k.t_mods])


def norm_mod_T(k, x_sub, A, B, tmp, hb, hT, s, tb0, tb1, plain=None):
    P = k.P
    if plain is None:
        ssq = k.small.ap[:, 0:1]
        rstd = k.small.ap[:, 1:2]
        P.act(tmp.ap, x_sub, AF.Square, [k.x_t], [tmp.t, k.small.t], accum_out=ssq)
        P.act(rstd, ssq, AF.Sqrt, [k.small.t], [k.small.t], scale=1.0 / D, bias=EPS)
        P.recip(rstd, rstd, [k.small.t], [k.small.t])
        P.stt(DVE, tmp.ap, x_sub, rstd, A.ap, ALU.mult, ALU.mult, [k.x_t, k.small.t, A.t], [tmp.t])
        P.tt(DVE, hb.ap, tmp.ap, B.ap, ALU.add, [tmp.t, B.t], [hb.t])
        src = hb.ap
        src_t = hb.t
    else:
        src, src_t = plain
    for half in range(2):
        b = tb0 if half == 0 else tb1
        pv = bank_bf(k, b).rearrange("p (a b) -> p a b", a=8)
        for c in range(8):
            cc = half * 8 + c
            P.tr(pv[:, c, :], src[:, cc * 128:(cc + 1) * 128], k.identb.ap, [src_t, k.identb.t], [k.bank_t[b]])
        eng = DVE if half == 0 else ACT
        P.cp(eng, hT.ap[:, half * 8:(half + 1) * 8, s * 128:(s + 1) * 128], pv, [k.bank_t[b]], [hT.t])


def phase_A(k, l, g):
    P, ar = k.P, k.ar
    ar.reset()
    P.barrier()
    A = ar.f32(D)
    B = ar.f32(D)
    tmp = ar.f32(D)
    xt = ar.f32(4 * D, (4, D))
    hb = [ar.bf16(D) for _ in range(2)]
    hT = ar.bf16(16 * 512, (16, 512))
    stg = [ar.f32(4 * 512, (4, 512)) for _ in range(2)]
    stgT = [ar.bf16(4 * 512, (4, 512)) for _ in range(2)]
    make_AB(k, l, g, 1, 0, "norm1", A, B, tmp)
    k.x_t = xt.t
    w_in = k.i["ev_w_in"] if l == 0 else k.i["od_w_in"]
    ncols = EV_COLS if l == 0 else OD_COLS
    fm_blocks = 0 if l == 0 else 4
    tm_start = 0 if l == 0 else 1024
    x_src = g.x_in if l == 0 else g.xres
    it = 0
    for tile in range(g.NT // 512):
        rows = slice(tile * 512, (tile + 1) * 512)
        R = [] if l == 0 else [g.t_xres[tile]]
        P.dma(SP, xt.ap, x_src[rows, :].rearrange("(s p) d -> p s d", p=128), R, [xt.t], "x")
        for s in range(4):
            norm_mod_T(k, xt.ap[:, s, :], A, B, tmp, hb[s % 2], hT, s, 0, 1)
        for fb in range(fm_blocks):
            wt = wload(k, w_in[:, fb * 512:(fb + 1) * 512], ("in", l, fb * 512))
            st = stgT[it % 2]
            for j in range(4):
                b = 2 + (it * 4 + j) % 6
                pb = bank(k, b)
                for c in range(16):
                    P.mm(pb, wt.ap[:, c, j * 128:(j + 1) * 128], hT.ap[:, c, :], c == 0, c == 15, [wt.t, hT.t], [k.bank_t[b]])
                P.cp(ACT if j % 2 == 0 else DVE, st.ap[:, j, :], pb, [k.bank_t[b]], [st.t])
            P.dma(SP, g.nqkT[fb * 512:(fb + 1) * 512, rows].rearrange("(j p) t -> p j t", p=128), st.ap, [st.t], [g.t_nqkT], f"sT{it % 2}")
            it += 1
        c0 = tm_start
        while c0 < ncols:
            n = min(512, ncols - c0)
            wt = wload(k, w_in[:, c0:c0 + n], ("in", l, c0))
            st = stg[it % 2]
            for s in range(4):
                b = 2 + (it * 4 + s) % 6
                pb = bank(k, b)
                for c in range(16):
                    P.mm(pb[:, 0:n], hT.ap[:, c, s * 128:(s + 1) * 128], wt.ap[:, c, 0:n], c == 0, c == 15, [wt.t, hT.t], [k.bank_t[b]])
                P.cp(ACT if s % 2 == 0 else DVE, st.ap[:, s, 0:n], pb[:, 0:n], [k.bank_t[b]], [st.t])
            P.dma(SP, g.zin[rows, c0:c0 + n].rearrange("(s p) n -> p s n", p=128), st.ap[:, :, 0:n], [st.t], [g.t_zin], f"st{it % 2}")
            it += 1
            c0 += n


def phase_C(k, l, g):
    P, ar = k.P, k.ar
    ar.reset()
    P.barrier()
    G1 = ar.f32(D)
    A = ar.f32(D)
    B = ar.f32(D)
    tmp = ar.f32(D)
    xt = ar.f32(4 * D, (4, D))
    mixb = ar.bf16(4 * D, (4, D))
    hT = ar.bf16(16 * 512, (16, 512))
    hid = ar.bf16(32 * 512, (32, 512))
    rl = [ar.f32(512) for _ in range(2)]
    k.x_t = xt.t
    make_AB(k, l, g, 4, 3, "norm2", A, B, tmp)
    w_out = k.i["ev_w_out"] if l == 0 else k.i["od_w_out"]
    x_src = g.x_in if l == 0 else g.xres
    it = 0
    for tile in range(g.NT // 512):
        rows = slice(tile * 512, (tile + 1) * 512)
        R = [] if l == 0 else [g.t_xres[tile]]
        P.dma(SP, xt.ap, x_src[rows, :].rearrange("(s p) d -> p s d", p=128), R, [xt.t], "x")
        P.dma(SP, mixb.ap, g.mix[rows, :].rearrange("(s p) d -> p s d", p=128), [g.t_mix], [mixb.t], "mx")
        load_bc(k, G1, mod_row(k, l, g, 2), [k.t_mods])
        for s in range(4):
            norm_mod_T(k, None, None, None, None, None, hT, s, 0, 1, plain=(mixb.ap[:, s, :], mixb.t))
        for nb in range(4):
            wt = wload(k, w_out[:, nb * 512:(nb + 1) * 512], ("out", l, nb))
            for s in range(4):
                b = 2 + (it % 6)
                it += 1
                pb = bank(k, b)
                for c in range(16):
                    P.mm(pb, hT.ap[:, c, s * 128:(s + 1) * 128], wt.ap[:, c, :], c == 0, c == 15, [wt.t, hT.t], [k.bank_t[b]])
                r = rl[it % 2]
                P.tt(DVE, r.ap, pb, G1.ap[:, nb * 512:(nb + 1) * 512], ALU.mult, [k.bank_t[b], G1.t], [r.t])
                P.tt(DVE, xt.ap[:, s, nb * 512:(nb + 1) * 512], xt.ap[:, s, nb * 512:(nb + 1) * 512], r.ap, ALU.add, [r.t, xt.t], [xt.t])
        for s in range(4):
            hbt = Tile(mixb.ap[:, s % 2, :], mixb.t)
            norm_mod_T(k, xt.ap[:, s, :], A, B, tmp, hbt, hT, s, 0, 1)
        load_bc(k, G1, mod_row(k, l, g, 5), [k.t_mods])
        for half in range(2):
            for fb in range(8):
                f0 = half * 4096 + fb * 512
                wt = wload(k, k.i["w_mlp1"][l, :, f0:f0 + 512], ("m1", l, f0))
                for j in range(4):
                    b = 2 + (it % 6)
                    it += 1
                    pb = bank(k, b)
                    for c in range(16):
                        P.mm(pb, wt.ap[:, c, j * 128:(j + 1) * 128], hT.ap[:, c, :], c == 0, c == 15, [wt.t, hT.t], [k.bank_t[b]])
                    r = rl[it % 2]
                    P.act(r.ap, pb, AF.Relu, [k.bank_t[b]], [r.t])
                    P.tt(DVE, hid.ap[:, fb * 4 + j, :], r.ap, r.ap, ALU.mult, [r.t], [hid.t])
            for nb in range(4):
                bs = [2, 3, 4, 5] if nb % 2 == 0 else [6, 7, 0, 1]
                for kq in range(2):
                    r0 = half * 4096 + kq * 2048
                    wt = wload(k, k.i["w_mlp2"][l, r0:r0 + 2048, nb * 512:(nb + 1) * 512], ("m2", l, r0, nb))
                    for c in range(16):
                        for s in range(4):
                            P.mm(bank(k, bs[s]), hid.ap[:, kq * 16 + c, s * 128:(s + 1) * 128], wt.ap[:, c, :],
                                 kq == 0 and c == 0, kq == 1 and c == 15, [wt.t, hid.t], [k.bank_t[bs[s]]])
                for s in range(4):
                    r = rl[s % 2]
                    P.tt(DVE, r.ap, bank(k, bs[s]), G1.ap[:, nb * 512:(nb + 1) * 512], ALU.mult, [k.bank_t[bs[s]], G1.t], [r.t])
                    P.tt(DVE, xt.ap[:, s, nb * 512:(nb + 1) * 512], xt.ap[:, s, nb * 512:(nb + 1) * 512], r.ap, ALU.add, [r.t, xt.t], [xt.t])
        if l == 0:
            P.dma(SP, g.xres[rows, :].rearrange("(s p) d -> p s d", p=128), xt.ap, [xt.t], [g.t_xres[tile]], "xo")
        else:
            load_bc(k, A, k.i["norm_f"][0:1, :])
            for s in range(4):
                ssq = k.small.ap[:, 0:1]
                rstd = k.small.ap[:, 1:2]
                P.act(tmp.ap, xt.ap[:, s, :], AF.Square, [xt.t], [tmp.t, k.small.t], accum_out=ssq)
                P.act(rstd, ssq, AF.Sqrt, [k.small.t], [k.small.t], scale=1.0 / D, bias=EPS)
                P.recip(rstd, rstd, [k.small.t], [k.small.t])
                P.stt(DVE, xt.ap[:, s, :], xt.ap[:, s, :], rstd, A.ap, ALU.mult, ALU.mult, [xt.t, k.small.t, A.t], [xt.t])
            P.dma(SP, g.y[rows, :].rearrange("(s p) d -> p s d", p=128), xt.ap, [xt.t], [], "xo")
            if tile + 1 < g.NT // 512:
                make_AB(k, l, g, 4, 3, "norm2", A, B, tmp)


def mixer_even(k, g, sq):
    gla_part(k, g, sq)
    gqa_part(k, g, sq)


def gla_part(k, g, sq):
    t0, T, is_s, si = sq
    P, ar = k.P, k.ar
    ar.reset(mixer=True)
    P.barrier()
    NTL = T // 128
    zin = g.zin
    tri = ar.f32(4 * 128, (4, 128))
    msk = ar.f32(2 * 128, (2, 128))
    wa2 = ar.f32(2 * 512, (2, 512))
    gn = ar.f32(128)
    gloT = ar.f32(128)
    S = ar.f32(1024, (8, 128))
    Sb = ar.bf16(1024, (8, 128))
    ofw = ar.f32(NTL * 1024, (NTL, 1024))
    qk = ar.f32(1024)
    vb = ar.bf16(1024, (8, 128))
    glo = ar.f32(32)
    e1 = ar.f32(512)
    sp = ar.f32(512)
    eb, enb, ekd = ar.f32(512), ar.f32(512), ar.f32(512)
    qe, ke, kd = ar.bf16(512), ar.bf16(512), ar.bf16(512)
    qeT = ar.bf16(1024, (8, 128))
    keT = ar.bf16(1024, (8, 128))
    AT = ar.bf16(1024, (8, 128))
    dec = ar.f32(8)
    tmpS = ar.f32(1024, (8, 128))
    gg = ar.f32(1024)
    ot = ar.f32(1024, (8, 128))
    sqb = ar.f32(1024, (8, 128))
    ssq = ar.f32(8)
    mo = ar.bf16(1024, (8, 128))
    P.dma(SP, tri.ap, k.i["tri"][0:4].rearrange("a p f -> p a f"), [], [tri.t], "c0")
    P.dma(SP, msk.ap, k.i["tri"][4:6].rearrange("a p f -> p a f"), [], [msk.t], "c1")
    P.dma(SP, wa2.ap[0:17], k.i["wa2"].rearrange("d r n -> r d n"), [], [wa2.t], "c2")
    load_bc(k, gn, k.i["gla_norm"][0:1, :])
    P.memset(DVE, gloT.ap[0:32, :], 1.0, [gloT.t])
    onec = k.ones_f.ap[:, 0:1]
    bt = k.bank_t
    at_v = bank(k, 3, 2).rearrange("p (h c) -> p h c", h=8)
    o_v = bank(k, 5, 2).rearrange("p (h c) -> p h c", h=8)
    kv_v = bank(k, 1, 2).rearrange("p (h c) -> p h c", h=8)
    trq = bank_bf(k, 1).rearrange("p (h c) -> p h c", h=8)
    trk_ = bank_bf(k, 2).rearrange("p (h c) -> p h c", h=8)
    for dr in range(2):
        if is_s:
            P.dma(SP, S.ap[0:64], k.i["sgla"][dr].rearrange("h d v -> d h v"), [], [S.t], "c3")
        else:
            P.memset(DVE, S.ap[0:64], 0.0, [S.t])
        P.cp(ACT, Sb.ap[0:64], S.ap[0:64], [S.t], [Sb.t])
        order = range(NTL) if dr == 0 else range(NTL - 1, -1, -1)
        for ti in order:
            rows = slice(t0 + ti * 128, t0 + (ti + 1) * 128)
            P.dma(SP, qk.ap, zin[rows, 0:1024], [g.t_zin], [qk.t], "ld0")
            P.dma(POOL, vb.ap.rearrange("p h c -> p (h c)"), zin[rows, 1024:2048], [g.t_zin], [vb.t], "ld1")
            P.dma(SP, glo.ap, zin[rows, 3072:3104], [g.t_zin], [glo.t], "ld2")
            P.tr(bank(k, 7)[0:16, 0:128], glo.ap[:, dr * 16:(dr + 1) * 16], k.identf.ap, [glo.t, k.identf.t], [bt[7]])
            P.cp(DVE, gloT.ap[0:16, :], bank(k, 7)[0:16, 0:128], [bt[7]], [gloT.t])
            P.mm(bank(k, 0), gloT.ap[0:17, :], wa2.ap[0:17, dr, :], True, True, [gloT.t, wa2.t], [bt[0]])
            P.act(e1.ap, bank(k, 0), AF.Exp, [bt[0]], [e1.t], scale=-1.0)
            P.act(sp.ap, e1.ap, AF.Ln, [e1.t], [sp.t], bias=1.0)
            P.mm(bank(k, 1), tri.ap[:, 2 * dr, :], sp.ap, True, True, [tri.t, sp.t], [bt[1]])
            P.mm(bank(k, 2), tri.ap[:, 2 * dr + 1, :], sp.ap, True, True, [tri.t, sp.t], [bt[2]])
            P.act(eb.ap, bank(k, 1), AF.Exp, [bt[1]], [eb.t], scale=-1.0 / 16)
            P.act(enb.ap, bank(k, 1), AF.Exp, [bt[1]], [enb.t], scale=1.0 / 16)
            P.act(ekd.ap, bank(k, 2), AF.Exp, [bt[2]], [ekd.t], scale=-1.0 / 16)
            P.stt(DVE, qe.ap, qk.ap[:, 0:512], 0.125, eb.ap, ALU.mult, ALU.mult, [qk.t, eb.t], [qe.t])
            P.tt(POOL, ke.ap, qk.ap[:, 512:1024], enb.ap, ALU.mult, [qk.t, enb.t], [ke.t])
            P.tt(POOL, kd.ap, qk.ap[:, 512:1024], ekd.ap, ALU.mult, [qk.t, ekd.t], [kd.t])
            for h in range(8):
                P.mm(bank(k, 0)[0:64, h:h + 1], sp.ap[:, h * 64:(h + 1) * 64], onec, True, True, [sp.t, k.ones_f.t], [bt[0]])
            P.act(dec.ap[0:64, :], bank(k, 0)[0:64, 0:8], AF.Exp, [bt[0]], [dec.t], scale=-1.0 / 16)
            P.tt(POOL, tmpS.ap[0:64], S.ap[0:64], bc_last(dec.ap[0:64, :], 128), ALU.mult, [S.t, dec.t], [tmpS.t])
            for h in range(8):
                P.tr(trq[0:64, h, :], qe.ap[:, h * 64:(h + 1) * 64], k.identb.ap, [qe.t, k.identb.t], [bt[1]])
            P.cp(DVE, qeT.ap[0:64], trq[0:64], [bt[1]], [qeT.t])
            for h in range(8):
                P.tr(trk_[0:64, h, :], ke.ap[:, h * 64:(h + 1) * 64], k.identb.ap, [ke.t, k.identb.t], [bt[2]])
            P.cp(ACT, keT.ap[0:64], trk_[0:64], [bt[2]], [keT.t])
            for h in range(8):
                P.mm(at_v[:, h, :], keT.ap[0:64, h, :], qeT.ap[0:64, h, :], True, True, [keT.t, qeT.t], [bt[3], bt[4]])
            P.tt(DVE, AT.ap, at_v, bc_mid(msk.ap[:, dr, :], 8), ALU.mult, [bt[3], bt[4], msk.t], [AT.t])
            for h in range(8):
                P.mm(o_v[:, h, :], AT.ap[:, h, :], vb.ap[:, h, :], True, False, [AT.t, vb.t], [bt[5], bt[6]])
                P.mm(o_v[:, h, :], qeT.ap[0:64, h, :], Sb.ap[0:64, h, :], False, True, [qeT.t, Sb.t], [bt[5], bt[6]])
            for h in range(8):
                P.mm(kv_v[0:64, h, :], kd.ap[:, h * 64:(h + 1) * 64], vb.ap[:, h, :], True, True, [kd.t, vb.t], [bt[1], bt[2]])
            P.tt(DVE, S.ap[0:64], tmpS.ap[0:64], kv_v[0:64], ALU.add, [tmpS.t, bt[1], bt[2]], [S.t])
            P.cp(ACT, Sb.ap[0:64], S.ap[0:64], [S.t], [Sb.t])
            if dr == 0:
                P.cp(ACT, ofw.ap[:, ti, :], bank(k, 5, 2), [bt[5], bt[6]], [ofw.t])
            else:
                P.tt(DVE, ot.ap, o_v, ofw.ap[:, ti, :].rearrange("p (h c) -> p h c", h=8), ALU.add, [bt[5], bt[6], ofw.t], [ot.t])
                head_norm_gate(k, ot, sqb, ssq, gn, gg, mo, zin[rows, 2048:3072], g, g.mix[rows, 0:1024])
        if not is_s:
            P.dma(SP, k.o["nsg"][si, dr].rearrange("h d v -> d h v"), S.ap[0:64], [S.t], [], "so")


def head_norm_gate(k, ot, sqb, ssq, gn, gg, mo, gate_src, g, mix_dst, lane=""):
    P = k.P
    P.act(sqb.ap, ot.ap, AF.Square, [ot.t], [sqb.t])
    P.red(ssq.ap, sqb.ap, ALU.add, [sqb.t], [ssq.t])
    P.act(ssq.ap, ssq.ap, AF.Sqrt, [ssq.t], [ssq.t], scale=1.0 / 128, bias=EPS)
    P.recip(ssq.ap, ssq.ap, [ssq.t], [ssq.t])
    P.tt(DVE, ot.ap, ot.ap, bc_last(ssq.ap, 128), ALU.mult, [ot.t, ssq.t], [ot.t])
    P.tt(DVE, ot.ap, ot.ap, bc_mid(gn.ap, 8), ALU.mult, [ot.t, gn.t], [ot.t])
    P.dma(SP, gg.ap, gate_src, [g.t_zin], [gg.t], "ld3" + lane)
    P.act(gg.ap, gg.ap, AF.Silu, [gg.t], [gg.t])
    P.tt(DVE, mo.ap, ot.ap, gg.ap.rearrange("p (h c) -> p h c", h=8), ALU.mult, [ot.t, gg.t], [mo.t])
    P.dma(SP, mix_dst, mo.ap.rearrange("p h c -> p (h c)"), [mo.t], [g.t_mix], "mo" + lane)


def attention(k, g, qT, kT, NK, V, vcol, nheads, kvmap, qt, scale, Pb, PT, st, mo):
    P = k.P
    bt = k.bank_t
    NB = (NK + 127) // 128
    S_ps = bank(k, 0, 5)
    nsb = (NK + 511) // 512
    for h in range(nheads):
        gq = kvmap(h)
        for kc in range(0, NK, 512):
            n = min(512, NK - kc)
            P.mm(S_ps[:, kc:kc + n], qT.ap[:, h, qt * 128:(qt + 1) * 128], kT.ap[:, gq, kc:kc + n], True, True,
                 [qT.t, kT.t], [bt[kc // 512]])
        sbt = [bt[i] for i in range(nsb)]
        P.op(DVE, lambda e, o_=st.ap[:, 0:1], i_=S_ps[:, 0:NK]: e.tensor_reduce(out=o_, in_=i_, axis=AX.X, op=ALU.max), sbt, [st.t])
        P.ts(DVE, st.ap[:, 1:2], st.ap[:, 0:1], -scale, None, ALU.mult, None, [st.t], [st.t])
        P.act(Pb.ap[:, 0:NK], S_ps[:, 0:NK], AF.Exp, sbt + [st.t], [Pb.t, st.t], scale=scale, bias=st.ap[:, 1:2], accum_out=st.ap[:, 2:3])
        P.recip(st.ap[:, 3:4], st.ap[:, 2:3], [st.t], [st.t])
        for kb0 in range(0, NB, 8):
            nb = min(8, NB - kb0)
            b = 5 + (kb0 // 8) % 2
            pv = bank_bf(k, b).rearrange("p (a c) -> p a c", a=8)
            for i in range(nb):
                kb = kb0 + i
                P.tr(pv[:, i, :], Pb.ap[:, kb * 128:(kb + 1) * 128], k.identb.ap, [Pb.t, k.identb.t], [bt[b]])
            P.cp(DVE if (kb0 // 8) % 2 == 0 else ACT, PT.ap[:, kb0:kb0 + nb, :], pv[:, 0:nb, :], [bt[b]], [PT.t])
        o_ps = bank(k, 7)[:, 0:128]
        for kb in range(NB):
            P.mm(o_ps, PT.ap[:, kb, :], V.ap[:, kb, vcol(gq)], kb == 0, kb == NB - 1, [PT.t, V.t], [bt[7]])
        P.op(ACT, lambda e, o_=mo.ap[:, h, :], i_=o_ps, m_=st.ap[:, 3:4]: e.mul(out=o_, in_=i_, mul=m_), [bt[7], st.t], [mo.t])


def gqa_part(k, g, sq):
    t0, T, is_s, si = sq
    P, ar = k.P, k.ar
    ar.reset(mixer=True)
    P.barrier()
    NTL = T // 128
    zin = g.zin
    off = PAST if is_s else 0
    NK = T + off
    NB = NK // 128
    bt = k.bank_t
    gain = ar.f32(1280, (10, 128))
    qT = ar.bf16(8 * T, (8, T))
    kT = ar.bf16(2 * NK, (2, NK))
    V = ar.bf16(NB * 258, (NB, 258))
    aqk = ar.f32(1280, (10, 128))
    sqb = ar.f32(1280, (10, 128))
    ss = ar.f32(16)
    xr = ar.bf16(1280, (10, 128))
    rp = ar.f32(128)
    t1 = ar.f32(640, (10, 2, 32))
    t2 = ar.f32(640, (10, 2, 32))
    QN = min(512, T)
    NQ = QN // 128
    PTb = [ar.bf16(QN) for _ in range(2)]
    mo = ar.bf16(NQ * 1024, (NQ, 8, 128))
    st = ar.f32(8)
    P.memset(DVE, V.ap, 1.0, [V.t])
    P.dma(SP, gain.ap.rearrange("p a c -> p (a c)"), k.i["qk_gain"][0:1, :].partition_broadcast(128), [], [gain.t], "bc")
    tr0 = bank_bf(k, 0).rearrange("p (a c) -> p a c", a=8)
    tr1 = bank_bf(k, 1).rearrange("p (a c) -> p a c", a=8)
    if is_s:
        for cb in range(2):
            ck = Tile(aqk.ap.rearrange("p a c -> p (a c)")[:, 0:256], aqk.t)
            P.dma(SP, ck.ap, k.i["cgk"][cb * 128:(cb + 1) * 128, :], [], [ck.t], "ld0")
            ckb = Tile(xr.ap.rearrange("p a c -> p (a c)")[:, 0:256], xr.t)
            P.cp(DVE, ckb.ap, ck.ap, [ck.t], [ckb.t])
            for gk in range(2):
                P.tr(tr1[:, gk, :], ckb.ap[:, gk * 128:(gk + 1) * 128], k.identb.ap, [ckb.t, k.identb.t], [bt[1]])
            P.cp(ACT, kT.ap[:, :, cb * 128:(cb + 1) * 128], tr1[:, 0:2, :], [bt[1]], [kT.t])
            P.dma(POOL, V.ap[:, cb, :].rearrange("p (g c) -> p g c", c=129)[:, :, 0:128], k.i["cgv"][cb * 128:(cb + 1) * 128, :].rearrange("p (g c) -> p g c", c=128), [], [V.t], "ld1")
    for ti in range(NTL):
        rows = slice(t0 + ti * 128, t0 + (ti + 1) * 128)
        aflat = aqk.ap.rearrange("p a c -> p (a c)")
        P.dma(SP, aflat, zin[rows, 3104:4384], [g.t_zin], [aqk.t], "ld0")
        P.dma(POOL, V.ap[:, off // 128 + ti, :].rearrange("p (g c) -> p g c", c=129)[:, :, 0:128], zin[rows, 4384:4640].rearrange("p (g c) -> p g c", c=128), [g.t_zin], [V.t], "ld1")
        P.act(sqb.ap, aqk.ap, AF.Square, [aqk.t], [sqb.t])
        P.red(ss.ap[:, 0:10], sqb.ap, ALU.add, [sqb.t], [ss.t])
        P.act(ss.ap[:, 0:10], ss.ap[:, 0:10], AF.Sqrt, [ss.t], [ss.t], scale=1.0 / 128, bias=EPS)
        P.recip(ss.ap[:, 0:10], ss.ap[:, 0:10], [ss.t], [ss.t])
        P.tt(DVE, aqk.ap, aqk.ap, bc_last(ss.ap[:, 0:10], 128), ALU.mult, [aqk.t, ss.t], [aqk.t])
        P.tt(POOL, aqk.ap, aqk.ap, gain.ap, ALU.mult, [aqk.t, gain.t], [aqk.t])
        if not is_s:
            P.dma(SP, k.o["ngk"][rows, :], aflat[:, 1024:1280], [aqk.t], [], "so")
            P.dma(SP, k.o["ngv"][rows, :], zin[rows, 4384:4640], [g.t_zin], [], "so2")
            P.cp(ACT, xr.ap, aqk.ap, [aqk.t], [xr.t])
        else:
            P.dma(SP, rp.ap, k.i["rope"][ti * 128:(ti + 1) * 128, :], [], [rp.t], "ld2")
            xv = aqk.ap.rearrange("p a (i j f) -> p a i j f", i=2, j=2, f=32)
            ov = xr.ap.rearrange("p a (i j f) -> p a i j f", i=2, j=2, f=32)
            x1, x2 = xv[:, :, :, 0, :], xv[:, :, :, 1, :]
            cs = rp.ap[:, 0:64].rearrange("p (i f) -> p i f", i=2).unsqueeze(1).to_broadcast([128, 10, 2, 32])
            sn = rp.ap[:, 64:128].rearrange("p (i f) -> p i f", i=2).unsqueeze(1).to_broadcast([128, 10, 2, 32])
            P.tt(DVE, t1.ap, x1, cs, ALU.mult, [aqk.t, rp.t], [t1.t])
            P.tt(POOL, t2.ap, x2, sn, ALU.mult, [aqk.t, rp.t], [t2.t])
            P.tt(DVE, ov[:, :, :, 0, :], t1.ap, t2.ap, ALU.subtract, [t1.t, t2.t], [xr.t])
            P.tt(DVE, t1.ap, x2, cs, ALU.mult, [aqk.t, rp.t], [t1.t])
            P.tt(POOL, t2.ap, x1, sn, ALU.mult, [aqk.t, rp.t], [t2.t])
            P.tt(DVE, ov[:, :, :, 1, :], t1.ap, t2.ap, ALU.add, [t1.t, t2.t], [xr.t])
        for hh in range(8):
            P.tr(tr0[:, hh, :], xr.ap[:, hh, :], k.identb.ap, [xr.t, k.identb.t], [bt[0]])
        for hh in range(2):
            P.tr(tr1[:, hh, :], xr.ap[:, 8 + hh, :], k.identb.ap, [xr.t, k.identb.t], [bt[1]])
        P.cp(DVE, qT.ap[:, :, ti * 128:(ti + 1) * 128], tr0, [bt[0]], [qT.t])
        P.cp(ACT, kT.ap[:, :, off + ti * 128:off + (ti + 1) * 128], tr1[:, 0:2, :], [bt[1]], [kT.t])
    scale = 128 ** -0.5
    it = 0
    PTb = PTb + [ar.bf16(QN) for _ in range(2)]
    for qg in range(T // QN):
        q0 = qg * QN
        for h in range(8):
            gq = h // 4
            def s_mm(kb_, slot):
                P.mm(bank(k, slot)[:, 0:QN], kT.ap[:, gq, kb_ * 128:(kb_ + 1) * 128], qT.ap[:, h, q0:q0 + QN], True, True,
                     [kT.t, qT.t], [bt[slot]])
            base = it
            s_mm(0, base % 4)
            if NB > 1:
                s_mm(1, (base + 1) % 4)
            for kb in range(NB):
                sb_ = (base + kb) % 4
                if kb + 2 < NB:
                    s_mm(kb + 2, (base + kb + 2) % 4)
                sT = bank(k, sb_)[:, 0:QN]
                pt = PTb[sb_]
                P.act(pt.ap, sT, AF.Exp, [bt[sb_]], [pt.t], scale=scale)
                for j in range(NQ):
                    P.mm(bank(k, 4 + j)[:, 0:129], pt.ap[:, j * 128:(j + 1) * 128], V.ap[:, kb, gq * 129:(gq + 1) * 129], kb == 0, kb == NB - 1,
                         [pt.t, V.t], [bt[4 + j]])
            it = base + NB
            for j in range(NQ):
                ops = bank(k, 4 + j)[:, 0:129]
                P.recip(st.ap[:, j:j + 1], ops[:, 128:129], [bt[4 + j]], [st.t])
                P.op(ACT, lambda e, o_=mo.ap[:, j, h, :], i_=ops[:, 0:128], m_=st.ap[:, j:j + 1]: e.mul(out=o_, in_=i_, mul=m_),
                     [bt[4 + j], st.t], [mo.t])
        rows = slice(t0 + q0, t0 + q0 + QN)
        P.dma(SP, g.mix[rows, 1024:2048].rearrange("(j p) c -> p j c", p=128), mo.ap.rearrange("p j h c -> p j (h c)"), [mo.t], [g.t_mix], "mo")


def mixer_odd(k, g, sq):
    na_part(k, g, sq)
    dn_part(k, g, sq)


def na_part(k, g, sq):
    t0, T, is_s, si = sq
    P, ar = k.P, k.ar
    ar.reset(mixer=True)
    P.barrier()
    NTL = T // 128
    zin = g.zin
    bt = k.bank_t
    off = PAST if is_s else 0
    NK = off + T
    scale = 128 ** -0.5
    qT = ar.bf16(8 * T, (8, T))
    kT = ar.bf16(8 * NK, (8, NK))
    mo = ar.bf16(1024, (8, 128))
    st = ar.f32(8)
    P.dma(SP, qT.ap, g.nqkT[0:1024, t0:t0 + T].rearrange("(h p) t -> p h t", p=128), [g.t_nqkT], [qT.t], "ld0")
    P.dma(SP, kT.ap[:, :, off:off + T], g.nqkT[1024:2048, t0:t0 + T].rearrange("(h p) t -> p h t", p=128), [g.t_nqkT], [kT.t], "ld1")
    if not is_s:
        Pb = ar.bf16(256)
        PT = ar.bf16(2 * 128, (2, 128))
        V = ar.bf16(2 * 1024, (2, 1024))
        rows_all = slice(t0, t0 + T)
        P.dma(POOL, V.ap, zin[rows_all, 2048:3072].rearrange("(b p) n -> p b n", p=128), [g.t_zin], [V.t], "ld2")
        P.dma(SP, k.o["nnk"][rows_all, :], zin[rows_all, 1024:2048], [g.t_zin], [], "so")
        P.dma(SP, k.o["nnv"][rows_all, :], zin[rows_all, 2048:3072], [g.t_zin], [], "so2")
        for qt in range(NTL):
            rows = slice(t0 + qt * 128, t0 + (qt + 1) * 128)
            attention(k, g, qT, kT, NK, V, lambda h: slice(h * 128, (h + 1) * 128), 8, lambda h: h, qt, scale, Pb, PT, st, mo)
            P.dma(SP, g.mix[rows, 0:1024], mo.ap.rearrange("p h c -> p (h c)"), [mo.t], [g.t_mix], "mo")
        return
    ckf = ar.f32(1024, (8, 128))
    ckb = ar.bf16(1024, (8, 128))
    Vc = ar.bf16(2 * 1024, (2, 1024))
    vband = ar.bf16(5 * 1024, (5, 1024))
    bias = ar.f32(8 * 576, (8, 576))
    mask = ar.f32(576)
    NB2 = [dict(Ssb=ar.f32(832), Pb=ar.bf16(832), PT=ar.bf16(7 * 128, (7, 128)), st=ar.f32(8)) for _ in range(2)]
    tr5 = bank_bf(k, 5).rearrange("p (a c) -> p a c", a=8)
    for cb in range(2):
        P.dma(SP, ckf.ap.rearrange("p h c -> p (h c)"), k.i["cnk"][cb * 128:(cb + 1) * 128, :], [], [ckf.t], "ld2")
        P.cp(DVE, ckb.ap, ckf.ap, [ckf.t], [ckb.t])
        for h in range(8):
            P.tr(tr5[:, h, :], ckb.ap[:, h, :], k.identb.ap, [ckb.t, k.identb.t], [bt[5]])
        P.cp(ACT, kT.ap[:, :, cb * 128:(cb + 1) * 128], tr5, [bt[5]], [kT.t])
    P.dma(POOL, Vc.ap, k.i["cnv"].rearrange("(b p) n -> p b n", p=128), [], [Vc.t], "ld3")
    segs = [(0, 128), (128, 128), (256, 128), (384, 128), (512, 64), (576, 128), (704, 128)]
    cur_pat = -1
    S_ps = bank(k, 0, 2)
    for j in range(NTL):
        rows = slice(t0 + j * 128, t0 + (j + 1) * 128)
        pat, kb = na_pat(j), na_kb(j)
        b0 = t0 + kb * 64
        P.dma(POOL, vband.ap[:, 0:4, :], zin[b0:b0 + 512, 2048:3072].rearrange("(b p) n -> p b n", p=128), [g.t_zin], [vband.t], "ld4")
        P.dma(POOL, vband.ap[0:64, 4, :], zin[b0 + 512:b0 + 576, 2048:3072], [g.t_zin], [vband.t], "ld5")
        if pat != cur_pat:
            cur_pat = pat
            P.dma(SP, bias.ap.rearrange("p h c -> p (h c)"), k.i["na_bias"][pat], [], [bias.t], "ld6")
            P.dma(SP, mask.ap, k.i["na_mask"][pat], [], [mask.t], "ld7")
            P.tt(DVE, bias.ap, bias.ap, bc_mid(mask.ap, 8), ALU.add, [bias.t, mask.t], [bias.t])
        kofs = off + kb * 64

        def stage_a1(h):
            p = h % 2
            B = NB2[p]
            S_ps = bank(k, 2 * p, 2)
            bA, bB = bt[2 * p], bt[2 * p + 1]
            qs = qT.ap[:, h, j * 128:(j + 1) * 128]
            P.mm(S_ps[:, 0:512], qs, kT.ap[:, h, kofs:kofs + 512], True, True, [qT.t, kT.t], [bA])
            P.mm(S_ps[:, 512:576], qs, kT.ap[:, h, kofs + 512:kofs + 576], True, True, [qT.t, kT.t], [bB])
            P.mm(S_ps[:, 576:832], qs, kT.ap[:, h, 0:256], True, True, [qT.t, kT.t], [bB])
            Ssb, Pb, st = B["Ssb"], B["Pb"], B["st"]
            P.stt(DVE, Ssb.ap[:, 0:576], S_ps[:, 0:576], scale, bias.ap[:, h, :], ALU.mult, ALU.add, [bA, bB, bias.t], [Ssb.t])
            P.op(ACT, lambda e, o_=Ssb.ap[:, 576:832], i_=S_ps[:, 576:832]: e.mul(out=o_, in_=i_, mul=scale), [bB], [Ssb.t])

        def stage_a2(h):
            p = h % 2
            B = NB2[p]
            Ssb, Pb, st = B["Ssb"], B["Pb"], B["st"]
            P.op(DVE, lambda e, o_=st.ap[:, 0:1], i_=Ssb.ap: e.tensor_reduce(out=o_, in_=i_, axis=AX.X, op=ALU.max), [Ssb.t], [st.t])
            P.ts(DVE, st.ap[:, 1:2], st.ap[:, 0:1], -1.0, None, ALU.mult, None, [st.t], [st.t])
            P.act(Pb.ap, Ssb.ap, AF.Exp, [Ssb.t, st.t], [Pb.t, st.t], bias=st.ap[:, 1:2], accum_out=st.ap[:, 2:3])
            P.recip(st.ap[:, 3:4], st.ap[:, 2:3], [st.t], [st.t])


        def stage_b1(h):
            p = h % 2
            B = NB2[p]
            Pb, PT, st = B["Pb"], B["PT"], B["st"]
            tb = 5 + p
            trv = bank_bf(k, tb).rearrange("p (a c) -> p a c", a=8)
            for i, (c0, n) in enumerate(segs):
                P.tr(trv[0:n, i, :], Pb.ap[:, c0:c0 + n], k.identb.ap, [Pb.t, k.identb.t], [bt[tb]])
            P.cp(DVE if p == 0 else ACT, PT.ap, trv[:, 0:7, :], [bt[tb]], [PT.t])

        def stage_b2(h):
            p = h % 2
            B = NB2[p]
            Pb, PT, st = B["Pb"], B["PT"], B["st"]
            ob = 7 if p == 0 else 4
            o_ps = bank(k, ob)[:, 0:128]
            for i, (c0, n) in enumerate(segs):
                if i < 5:
                    rhs = vband.ap[0:n, i, h * 128:(h + 1) * 128]
                    rt = vband.t
                else:
                    rhs = Vc.ap[:, i - 5, h * 128:(h + 1) * 128]
                    rt = Vc.t
                P.mm(o_ps, PT.ap[0:n, i, :], rhs, i == 0, i == 6, [PT.t, rt], [bt[ob]])
            P.op(ACT, lambda e, o_=mo.ap[:, h, :], i_=o_ps, m_=st.ap[:, 3:4]: e.mul(out=o_, in_=i_, mul=m_), [bt[ob], st.t], [mo.t])


        stage_a1(0)
        stage_a2(0)
        for h in range(8):
            if h + 1 < 8:
                stage_a1(h + 1)
            stage_b1(h)
            if h + 1 < 8:
                stage_a2(h + 1)
            stage_b2(h)
        P.dma(SP, g.mix[rows, 0:1024], mo.ap.rearrange("p h c -> p (h c)"), [mo.t], [g.t_mix], "mo")


def dn_part(k, g, sq):
    t0, T, is_s, si = sq
    P, ar = k.P, k.ar
    ar.reset(mixer=True)
    P.barrier()
    NTL = T // 128
    C = K()
    C.tri = ar.f32(4 * 128, (4, 128))
    C.mks = ar.f32(4 * 128, (4, 128))
    C.bm = ar.f32(3 * 128, (3, 128))
    C.negA = ar.f32(16)
    C.dtb = ar.f32(16)
    P.dma(SP, C.tri.ap, k.i["tri"][0:4].rearrange("a p f -> p a f"), [], [C.tri.t], "c0")
    P.dma(SP, C.mks.ap, k.i["tri"][4:8].rearrange("a p f -> p a f"), [], [C.mks.t], "c1")
    P.dma(SP, C.bm.ap, k.i["tri"][8:11].rearrange("a p f -> p a f"), [], [C.bm.t], "c2")
    load_bc(k, C.dtb, k.i["od_dtb"][0:1, :])
    load_bc(k, C.negA, k.i["od_alog"][0:1, :])
    P.act(C.negA.ap, C.negA.ap, AF.Exp, [C.negA.t], [C.negA.t])
    P.ts(DVE, C.negA.ap, C.negA.ap, -1.0, None, ALU.mult, None, [C.negA.t], [C.negA.t])
    gens = [dn_chain(k, g, sq, dr, C) for dr in range(2)]
    lead = 34 if NTL > 2 else 20
    alive = [True, True]
    step = 0
    while any(alive):
        for ci in range(2):
            if not alive[ci]:
                continue
            if ci == 1 and step < lead and alive[0]:
                continue
            try:
                next(gens[ci])
            except StopIteration:
                alive[ci] = False
        step += 1
    ar.reset(mixer=True)
    P.barrier()
    gn = ar.f32(128)
    load_bc(k, gn, k.i["dn_norm"][0:1, :])
    bufs = []
    for _ in range(2):
        bufs.append(dict(a=ar.f32(1024, (8, 128)), b=ar.f32(1024), sq=ar.f32(1024, (8, 128)), ss=ar.f32(8), gg=ar.f32(1024), mo=ar.bf16(1024, (8, 128))))
    for ti in range(NTL):
        rows = slice(t0 + ti * 128, t0 + (ti + 1) * 128)
        B = bufs[ti % 2]
        P.dma(SP, B["a"].ap.rearrange("p h c -> p (h c)"), g.ofw[rows, :], [g.t_ofw], [B["a"].t], f"ca{ti % 2}")
        P.dma(SP, B["b"].ap, g.obw[rows, :], [g.t_obw], [B["b"].t], f"cb{ti % 2}")
        P.tt(DVE, B["a"].ap, B["a"].ap, B["b"].ap.rearrange("p (h c) -> p h c", h=8), ALU.add, [B["a"].t, B["b"].t], [B["a"].t])
        head_norm_gate(k, B["a"], B["sq"], B["ss"], gn, B["gg"], B["mo"], g.zin[rows, 6144:7168], g, g.mix[rows, 1024:2048], lane=f"{ti % 2}")


def dn_chain(k, g, sq, dr, C):
    t0, T, is_s, si = sq
    P, ar = k.P, k.ar
    NTL = T // 128
    zin = g.zin
    bt = k.bank_t
    L = f"d{dr}"
    tri, mks, bm, negA, dtb = C.tri, C.mks, C.bm, C.negA, C.dtb
    wc = ar.f32(3 * 1024, (3, 1024))
    X0 = ar.f32(1024, (8, 128))
    Xm = ar.f32(1024, (8, 128))
    Xp = ar.f32(1024, (8, 128))
    yv = ar.f32(1024, (8, 128))
    tv = ar.f32(1024, (8, 128))
    ssq = ar.f32(8)
    dab = ar.f32(32)
    gs = ar.f32(96)
    Ls = ar.f32(1024, (8, 128))
    LT = ar.f32(1024, (8, 128))
    knb = ar.bf16(1024, (8, 128))
    qnb = ar.bf16(1024, (8, 128))
    qeb = ar.bf16(1024, (8, 128))
    kdb = ar.bf16(1024, (8, 128))
    kT = ar.bf16(1024, (8, 128))
    qT = ar.bf16(1024, (8, 128))
    qeT = ar.bf16(1024, (8, 128))
    Nf = ar.f32(1024, (8, 128))
    NTf = ar.f32(1024, (8, 128))
    mk = lambda: ar.f32(512, (4, 128))
    sb = dict(P=mk(), PT=mk(), T=mk(), TT=mk(), A=mk(), B=mk(), No=mk(), NoT=mk())
    Y = ar.f32(2048, (8, 256))
    S = ar.f32(1024, (8, 128))
    Sb = ar.bf16(1024, (8, 128))
    tmpS = ar.f32(1024, (8, 128))
    fl = lambda t_: t_.ap.rearrange("p h c -> p (h c)")
    x0b = fl(X0).bitcast(BF16)
    yvb = fl(yv).bitcast(BF16)
    nkc = Tile(x0b[:, 0:1024].rearrange("p (h c) -> p h c", h=8), X0.t)
    nkcT = Tile(x0b[:, 1024:2048].rearrange("p (h c) -> p h c", h=8), X0.t)
    ub = Tile(yvb[:, 0:1024].rearrange("p (h c) -> p h c", h=8), yv.t)
    attnT = Tile(yvb[:, 1024:2048].rearrange("p (h c) -> p h c", h=8), yv.t)
    dg, D1, et, ot = Xm, Xp, tv, Xm
    beta, gt, gc, ngc, eg, egl, decb, nbeta, beg = [gs.ap[:, i * 8:(i + 1) * 8] for i in range(9)]
    v8 = lambda b: bank(k, b, 2).rearrange("p (h c) -> p h c", h=8)
    v4 = lambda b: bank(k, b).rearrange("p (h c) -> p h c", h=4)
    trb = lambda b: bank_bf(k, b).rearrange("p (a c) -> p a c", a=8)
    ng_v = v8(1)
    b0 = bank(k, 0)
    mLs = mks.ap[:, 3 if dr == 0 else 2, :]
    mLT = mks.ap[:, 0 if dr == 0 else 1, :]
    o_dst, t_o = (g.ofw, g.t_ofw) if dr == 0 else (g.obw, g.t_obw)
    if is_s:
        P.dma(SP, S.ap, k.i["sdl"][dr].rearrange("h d v -> d h v"), [], [S.t], L + "s")
    else:
        P.memset(DVE, S.ap, 0.0, [S.t])
    P.cp(ACT, Sb.ap, S.ap, [S.t], [Sb.t])
    order = range(NTL) if dr == 0 else range(NTL - 1, -1, -1)
    for ti in order:
        r0 = t0 + ti * 128
        rows = slice(r0, r0 + 128)
        P.dma(SP, dab.ap, zin[rows, 7168:7200], [g.t_zin], [dab.t], L + "g")
        P.act(beta, dab.ap[:, 16 + dr * 8:24 + dr * 8], AF.Sigmoid, [dab.t], [gs.t])
        P.tt(DVE, gt, dab.ap[:, dr * 8:dr * 8 + 8], dtb.ap[:, dr * 8:dr * 8 + 8], ALU.add, [dab.t, dtb.t], [gs.t])
        P.act(gt, gt, AF.Exp, [gs.t], [gs.t])
        P.act(gt, gt, AF.Ln, [gs.t], [gs.t], bias=1.0)
        P.tt(DVE, gt, gt, negA.ap[:, dr * 8:dr * 8 + 8], ALU.mult, [gs.t, negA.t], [gs.t])
        P.mm(b0[:, 0:8], tri.ap[:, 2 * dr, :], gt, True, True, [tri.t, gs.t], [bt[0]])
        P.mm(b0[:, 8:16], tri.ap[:, 2 * dr + 1, :], gt, True, True, [tri.t, gs.t], [bt[0]])
        P.mm(b0[:, 16:24], k.ones_f.ap, gt, True, True, [k.ones_f.t, gs.t], [bt[0]])
        P.cp(DVE, gc, b0[:, 0:8], [bt[0]], [gs.t])
        P.act(eg, b0[:, 0:8], AF.Exp, [bt[0]], [gs.t])
        P.act(egl, b0[:, 8:16], AF.Exp, [bt[0]], [gs.t])
        P.act(decb, b0[:, 16:24], AF.Exp, [bt[0]], [gs.t])
        P.ts(DVE, ngc, gc, -1.0, None, ALU.mult, None, [gs.t], [gs.t])
        P.ts(DVE, nbeta, beta, -1.0, None, ALU.mult, None, [gs.t], [gs.t])
        P.tt(DVE, beg, beta, eg, ALU.mult, [gs.t], [gs.t])
        P.tt(POOL, tmpS.ap, S.ap, bc_last(decb, 128), ALU.mult, [S.t, gs.t], [tmpS.t])
        yield
        P.tt(POOL, dg.ap, bc_mid(k.identf.ap, 8), bc_last(ngc, 128), ALU.mult, [k.identf.t, gs.t], [dg.t])
        dgf = fl(dg)
        P.mm(bank(k, 1), k.ones_f.ap, dgf[:, 0:512], True, True, [k.ones_f.t, dg.t], [bt[1]])
        P.mm(bank(k, 2), k.ones_f.ap, dgf[:, 512:1024], True, True, [k.ones_f.t, dg.t], [bt[2]])
        P.tt(DVE, D1.ap, ng_v, bc_last(gc, 128), ALU.add, [bt[1], bt[2], gs.t], [D1.t])
        yield
        P.ts(DVE, et.ap, D1.ap, 0.0, None, ALU.min, None, [D1.t], [et.t])
        P.act(et.ap, et.ap, AF.Exp, [et.t], [et.t])
        P.tt(POOL, Ls.ap, et.ap, bc_mid(mLs, 8), ALU.mult, [et.t, mks.t], [Ls.t])
        yield
        P.ts(DVE, et.ap, D1.ap, 0.0, None, ALU.max, None, [D1.t], [et.t])
        P.act(et.ap, et.ap, AF.Exp, [et.t], [et.t], scale=-1.0)
        P.tt(POOL, LT.ap, et.ap, bc_mid(mLT, 8), ALU.mult, [et.t, mks.t], [LT.t])
        yield
        for gi in range(3):
            c0 = 3072 + gi * 1024
            cols = slice(c0, c0 + 1024)
            P.dma(SP, wc.ap, k.i["od_conv"][:, gi * 1024:(gi + 1) * 1024].rearrange("(o a) n -> o a n", o=1).partition_broadcast(128), [], [wc.t], L + "w")
            P.dma(SP, fl(X0), zin[rows, cols], [g.t_zin], [X0.t], L + "0")
            if ti == 0:
                P.memset(DVE, fl(Xm), 0.0, [Xm.t])
                P.dma(SP, fl(Xm)[1:128, :], zin[r0:r0 + 127, cols], [g.t_zin], [Xm.t], L + "1")
            else:
                P.dma(SP, fl(Xm), zin[r0 - 1:r0 + 127, cols], [g.t_zin], [Xm.t], L + "1")
            if ti == NTL - 1:
                P.memset(DVE, fl(Xp), 0.0, [Xp.t])
                P.dma(SP, fl(Xp)[0:127, :], zin[r0 + 1:r0 + 128, cols], [g.t_zin], [Xp.t], L + "2")
            else:
                P.dma(SP, fl(Xp), zin[r0 + 1:r0 + 129, cols], [g.t_zin], [Xp.t], L + "2")
            P.tt(DVE, fl(yv), fl(Xm), wc.ap[:, 0, :], ALU.mult, [Xm.t, wc.t], [yv.t])
            yield
            P.tt(DVE, fl(tv), fl(X0), wc.ap[:, 1, :], ALU.mult, [X0.t, wc.t], [tv.t])
            yield
            P.tt(DVE, fl(yv), fl(yv), fl(tv), ALU.add, [yv.t, tv.t], [yv.t])
            yield
            P.tt(DVE, fl(tv), fl(Xp), wc.ap[:, 2, :], ALU.mult, [Xp.t, wc.t], [tv.t])
            yield
            P.tt(DVE, fl(yv), fl(yv), fl(tv), ALU.add, [yv.t, tv.t], [yv.t])
            yield
            P.act(yv.ap, yv.ap, AF.Silu, [yv.t], [yv.t])
            yield
            if gi == 2:
                P.tt(DVE, Y.ap[:, :, 0:128], yv.ap, bc_last(beta, 128), ALU.mult, [yv.t, gs.t], [Y.t])
            else:
                P.act(tv.ap, yv.ap, AF.Square, [yv.t], [tv.t])
                P.red(ssq.ap, tv.ap, ALU.add, [tv.t], [ssq.t])
                P.act(ssq.ap, ssq.ap, AF.Sqrt, [ssq.t], [ssq.t], bias=EPS)
                P.recip(ssq.ap, ssq.ap, [ssq.t], [ssq.t])
                yield
                if gi == 0:
                    P.stt(DVE, tv.ap, yv.ap, 128 ** -0.5, bc_last(ssq.ap, 128), ALU.mult, ALU.mult, [yv.t, ssq.t], [tv.t])
                    P.cp(ACT, qnb.ap, tv.ap, [tv.t], [qnb.t])
                    P.tt(POOL, qeb.ap, tv.ap, bc_last(eg, 128), ALU.mult, [tv.t, gs.t], [qeb.t])
                else:
                    P.tt(DVE, tv.ap, yv.ap, bc_last(ssq.ap, 128), ALU.mult, [yv.t, ssq.t], [tv.t])
                    P.cp(ACT, knb.ap, tv.ap, [tv.t], [knb.t])
                    P.tt(POOL, kdb.ap, tv.ap, bc_last(egl, 128), ALU.mult, [tv.t, gs.t], [kdb.t])
                    P.tt(POOL, Y.ap[:, :, 128:256], tv.ap, bc_last(beg, 128), ALU.mult, [tv.t, gs.t], [Y.t])
            yield
        for h in range(8):
            P.tr(trb(7)[:, h, :], knb.ap[:, h, :], k.identb.ap, [knb.t, k.identb.t], [bt[7]])
        P.cp(DVE, kT.ap, trb(7), [bt[7]], [kT.t])
        yield
        for h in range(8):
            P.tr(trb(0)[:, h, :], qnb.ap[:, h, :], k.identb.ap, [qnb.t, k.identb.t], [bt[0]])
        P.cp(ACT, qT.ap, trb(0), [bt[0]], [qT.t])
        yield
        for h in range(8):
            P.tr(trb(7)[:, h, :], qeb.ap[:, h, :], k.identb.ap, [qeb.t, k.identb.t], [bt[7]])
        P.cp(DVE, qeT.ap, trb(7), [bt[7]], [qeT.t])
        yield
        for h in range(8):
            P.mm(ng_v[:, h, :], kT.ap[:, h, :], kT.ap[:, h, :], True, True, [kT.t], [bt[1], bt[2]])
        P.tt(DVE, Nf.ap, ng_v, Ls.ap, ALU.mult, [bt[1], bt[2], Ls.t], [Nf.t])
        yield
        P.tt(POOL, Nf.ap, Nf.ap, bc_last(nbeta, 128), ALU.mult, [Nf.t, gs.t], [Nf.t])
        for h in range(8):
            P.tr(ng_v[:, h, :], Nf.ap[:, h, :], k.identf.ap, [Nf.t, k.identf.t], [bt[1], bt[2]])
        P.cp(ACT, NTf.ap, ng_v, [bt[1], bt[2]], [NTf.t])
        yield
        for half in range(2):
            hs = slice(half * 4, half * 4 + 4)
            P.tt(POOL, sb["P"].ap, Nf.ap[:, hs, :], bc_mid(bm.ap[:, 0, :], 4), ALU.mult, [Nf.t, bm.t], [sb["P"].t])
            P.tt(DVE, sb["PT"].ap, NTf.ap[:, hs, :], bc_mid(bm.ap[:, 0, :], 4), ALU.mult, [NTf.t, bm.t], [sb["PT"].t])
            P.tt(DVE, sb["T"].ap, sb["P"].ap, bc_mid(k.identf.ap, 4), ALU.add, [sb["P"].t, k.identf.t], [sb["T"].t])
            P.tt(POOL, sb["TT"].ap, sb["PT"].ap, bc_mid(k.identf.ap, 4), ALU.add, [sb["PT"].t, k.identf.t], [sb["TT"].t])
            yield
            for lev in range(4):
                for hh in range(4):
                    P.mm(v4(3)[:, hh, :], sb["PT"].ap[:, hh, :], sb["P"].ap[:, hh, :], True, True, [sb["PT"].t, sb["P"].t], [bt[3]])
                for hh in range(4):
                    P.mm(v4(4)[:, hh, :], sb["P"].ap[:, hh, :], sb["PT"].ap[:, hh, :], True, True, [sb["PT"].t, sb["P"].t], [bt[4]])
                P.cp(ACT, sb["P"].ap, v4(3), [bt[3]], [sb["P"].t])
                P.cp(ACT, sb["PT"].ap, v4(4), [bt[4]], [sb["PT"].t])
                yield
                for hh in range(4):
                    P.mm(v4(5)[:, hh, :], sb["TT"].ap[:, hh, :], sb["P"].ap[:, hh, :], True, True, [sb["TT"].t, sb["P"].t], [bt[5]])
                for hh in range(4):
                    P.mm(v4(6)[:, hh, :], sb["P"].ap[:, hh, :], sb["TT"].ap[:, hh, :], True, True, [sb["TT"].t, sb["P"].t], [bt[6]])
                P.tt(DVE, sb["T"].ap, sb["T"].ap, v4(5), ALU.add, [sb["T"].t, bt[5]], [sb["T"].t])
                P.tt(DVE, sb["TT"].ap, sb["TT"].ap, v4(6), ALU.add, [sb["TT"].t, bt[6]], [sb["TT"].t])
                yield
            for mi in (1, 2):
                P.tt(POOL, sb["No"].ap, Nf.ap[:, hs, :], bc_mid(bm.ap[:, mi, :], 4), ALU.mult, [Nf.t, bm.t], [sb["No"].t])
                P.tt(DVE, sb["NoT"].ap, NTf.ap[:, hs, :], bc_mid(bm.ap[:, mi, :], 4), ALU.mult, [NTf.t, bm.t], [sb["NoT"].t])
                for hh in range(4):
                    P.mm(v4(3)[:, hh, :], sb["NoT"].ap[:, hh, :], sb["T"].ap[:, hh, :], True, True, [sb["NoT"].t, sb["T"].t], [bt[3]])
                for hh in range(4):
                    P.mm(v4(4)[:, hh, :], sb["No"].ap[:, hh, :], sb["TT"].ap[:, hh, :], True, True, [sb["No"].t, sb["TT"].t], [bt[4]])
                P.cp(ACT, sb["A"].ap, v4(3), [bt[3]], [sb["A"].t])
                P.cp(ACT, sb["B"].ap, v4(4), [bt[4]], [sb["B"].t])
                yield
                for hh in range(4):
                    P.mm(v4(5)[:, hh, :], sb["TT"].ap[:, hh, :], sb["A"].ap[:, hh, :], True, True, [sb["TT"].t, sb["A"].t], [bt[5]])
                for hh in range(4):
                    P.mm(v4(6)[:, hh, :], sb["T"].ap[:, hh, :], sb["B"].ap[:, hh, :], True, True, [sb["T"].t, sb["B"].t], [bt[6]])
                P.tt(DVE, sb["T"].ap, sb["T"].ap, v4(5), ALU.add, [sb["T"].t, bt[5]], [sb["T"].t])
                P.tt(DVE, sb["TT"].ap, sb["TT"].ap, v4(6), ALU.add, [sb["TT"].t, bt[6]], [sb["TT"].t])
                yield
            yv_ = bank(k, 3, 2).rearrange("p (h c) -> p h c", h=4)
            for hh in range(4):
                h = half * 4 + hh
                P.mm(yv_[:, hh, :], sb["TT"].ap[:, hh, :], Y.ap[:, h, :], True, True, [sb["TT"].t, Y.t], [bt[3], bt[4]])
            P.cp(ACT if half == 0 else DVE, Y.ap[:, hs, :], yv_, [bt[3], bt[4]], [Y.t])
            yield
        P.ts(DVE, nkc.ap, Y.ap[:, :, 128:256], -1.0, None, ALU.mult, None, [Y.t], [nkc.t])
        for h in range(8):
            P.tr(trb(7)[:, h, :], nkc.ap[:, h, :], k.identb.ap, [nkc.t, k.identb.t], [bt[7]])
        P.cp(ACT, nkcT.ap, trb(7), [bt[7]], [nkcT.t])
        for h in range(8):
            P.mm(ng_v[:, h, :], nkcT.ap[:, h, :], Sb.ap[:, h, :], True, True, [nkcT.t, Sb.t], [bt[1], bt[2]])
        P.tt(DVE, ub.ap, Y.ap[:, :, 0:128], ng_v, ALU.add, [Y.t, bt[1], bt[2]], [ub.t])
        yield
        for h in range(8):
            P.mm(ng_v[:, h, :], kT.ap[:, h, :], qT.ap[:, h, :], True, True, [kT.t, qT.t], [bt[1], bt[2]])
        P.tt(DVE, attnT.ap, ng_v, LT.ap, ALU.mult, [bt[1], bt[2], LT.t], [attnT.t])
        yield
        for h in range(8):
            P.mm(ng_v[:, h, :], qeT.ap[:, h, :], Sb.ap[:, h, :], True, False, [qeT.t, Sb.t], [bt[1], bt[2]])
            P.mm(ng_v[:, h, :], attnT.ap[:, h, :], ub.ap[:, h, :], False, True, [attnT.t, ub.t], [bt[1], bt[2]])
        P.cp(ACT, ot.ap, ng_v, [bt[1], bt[2]], [ot.t])
        P.dma(SP, o_dst[rows, :], fl(ot), [ot.t], [t_o], L + "o")
        yield
        for h in range(8):
            P.mm(ng_v[:, h, :], kdb.ap[:, h, :], ub.ap[:, h, :], True, True, [kdb.t, ub.t], [bt[1], bt[2]])
        P.tt(DVE, S.ap, tmpS.ap, ng_v, ALU.add, [tmpS.t, bt[1], bt[2]], [S.t])
        P.cp(ACT, Sb.ap, S.ap, [S.t], [Sb.t])
        yield
    if not is_s:
        P.dma(SP, k.o["nsd"][si, dr].rearrange("h d v -> d h v"), S.ap, [S.t], [], L + "so")


def na_pattern_j(p):
    return [0, 1, 2, 14, 15][p]


def na_kb(j):
    return min(max(2 * j - 4, 0), 23)


def na_pat(j):
    if j <= 1:
        return j
    if j <= 13:
        return 2
    return j - 11


_CONST = {}


def host_consts():
    if _CONST:
        return _CONST
    idx = np.arange(128)
    s, c = idx[:, None], idx[None, :]
    bd = lambda b: (s // b) == (c // b)
    tri = np.stack([s <= c, s > c, s >= c, s < c, s <= c, s >= c, s < c, s > c,
                    bd(32), bd(64) & ~bd(32), ~bd(64)]).astype(np.float32)
    t = np.arange(NS_TOK)
    inv = 1.0 / (10000.0 ** (np.arange(32, dtype=np.float32) / 32))
    pos = np.stack([t // 64, t % 64], axis=1).astype(np.float32)
    ang = pos[:, :, None] * inv
    rope = np.concatenate([np.cos(ang).reshape(NS_TOK, 64), np.sin(ang).reshape(NS_TOK, 64)], axis=1).astype(np.float32)
    gi = np.zeros((5, 128, 576, 2), np.int64)
    inwin = np.zeros((5, 128, 576), bool)
    for p in range(5):
        j = na_pattern_j(p)
        kb = na_kb(j)
        qi = np.arange(128)
        r = 2 * j + qi // 64
        cq = qi % 64
        ki = np.arange(576)
        kr = kb + ki // 64
        kc = ki % 64
        rs = np.clip(r - 4, 0, 24)
        cs = np.clip(cq - 8, 0, 48)
        okr = (kr[None, :] >= rs[:, None]) & (kr[None, :] < rs[:, None] + 8)
        okc = (kc[None, :] >= cs[:, None]) & (kc[None, :] < cs[:, None] + 16)
        ok = okr & okc
        inwin[p] = ok
        gi[p, :, :, 0] = np.where(ok, kr[None, :] - r[:, None] + 7, 0)
        gi[p, :, :, 1] = np.where(ok, kc[None, :] - cq[:, None] + 15, 0)
    _CONST.update(ident=np.eye(128, dtype=np.float32), tri=tri, rope=rope, gi=gi, inwin=inwin,
                  na_mask=np.where(inwin, 0.0, -30000.0).astype(np.float32))
    return _CONST


def prep_shared(inp):
    C = host_consts()
    f = lambda a: np.ascontiguousarray(a, dtype=np.float32)
    rpb = inp["od_rpb"][0]
    gath = rpb[:, C["gi"][..., 0], C["gi"][..., 1]]
    gath = np.where(C["inwin"][None], gath, np.float32(0))
    na_bias = f(np.transpose(gath, (1, 2, 0, 3)).reshape(5, 128, 8 * 576))
    sh = {
        "w_ada": f(inp["w_ada"]), "b_ada": f(inp["b_ada"]), "norm1": f(inp["norm1"]), "norm2": f(inp["norm2"]),
        "w_mlp1": f(inp["w_mlp1"]), "w_mlp2": f(inp["w_mlp2"]), "ev_w_in": f(inp["ev_w_in"][0]),
        "wa2": f(np.concatenate([inp["ev_w_a2"][0], inp["ev_b_a2"][0][:, None, :]], axis=1)),
        "gla_norm": f(inp["ev_gla_norm"][0:1]),
        "qk_gain": f(np.concatenate([np.tile(inp["ev_q_norm"][0], 8), np.tile(inp["ev_k_norm"][0], 2)])[None, :]),
        "ev_w_out": f(inp["ev_w_out"][0]), "od_w_in": f(inp["od_w_in"][0]), "od_conv": f(inp["od_conv"][0]),
        "od_alog": f(inp["od_a_log"][0].reshape(1, 16)), "od_dtb": f(inp["od_dt_bias"][0].reshape(1, 16)),
        "dn_norm": f(inp["od_dn_norm"][0:1]), "na_bias": na_bias, "na_mask": C["na_mask"],
        "od_w_out": f(inp["od_w_out"][0]), "norm_f": f(inp["norm_f"][None, :]),
        "ident": C["ident"], "tri": C["tri"], "rope": C["rope"],
    }
    return sh


def prep_core(inp, sh, i):
    f = lambda a: np.ascontiguousarray(a, dtype=np.float32)
    s = i // 2
    cond = np.stack([inp["c_ctx"], inp["c"][s]], axis=0)
    condT = f(cond.reshape(2, 16, 128).transpose(2, 1, 0).reshape(128, 32))
    m = dict(sh)
    m.update({
        "xp": f(inp["x_prompt"][2 * i:2 * i + 2].reshape(NP_TOK, D)), "xs": f(inp["x_sample"][s]), "condT": condT,
        "sgla": f(inp["state_gla"][s, 0]), "cgk": f(inp["cache_gqa_k"][s, 0].reshape(PAST, 256)),
        "cgv": f(inp["cache_gqa_v"][s, 0].reshape(PAST, 256)), "cnk": f(inp["cache_na_k"][s, 0].reshape(PAST, 1024)),
        "cnv": f(inp["cache_na_v"][s, 0].reshape(PAST, 1024)), "sdl": f(inp["state_delta"][s, 0]),
    })
    return m


_PROG = {}


def kernel(**inputs):
    inp = {k_: np.asarray(v) for k_, v in inputs.items()}
    if "nc" not in _PROG:
        _PROG["nc"] = build_program()
    nc = _PROG["nc"]
    sh = prep_shared(inp)
    in_maps = [prep_core(inp, sh, i) for i in range(NCORES)]
    res = run_bass_kernel_spmd(nc, in_maps, core_ids=list(range(NCORES))).results
    B = 16
    y_prompt = np.concatenate([r["yp"].reshape(2, 256, D) for r in res], axis=0)
    y_sample = np.stack([res[2 * s]["ys"] for s in range(4)], axis=0)
    st_gla = np.concatenate([r["nsg"].reshape(2, 1, 2, 8, 64, 128) for r in res], axis=0)
    ck_gqa = np.concatenate([r["ngk"].reshape(2, 1, 256, 2, 128) for r in res], axis=0)
    cv_gqa = np.concatenate([r["ngv"].reshape(2, 1, 256, 2, 128) for r in res], axis=0)
    ck_na = np.concatenate([r["nnk"].reshape(2, 1, 256, 8, 128) for r in res], axis=0)
    cv_na = np.concatenate([r["nnv"].reshape(2, 1, 256, 8, 128) for r in res], axis=0)
    st_dn = np.concatenate([r["nsd"].reshape(2, 1, 2, 8, 128, 128) for r in res], axis=0)
    outs = (y_prompt, y_sample, st_gla, ck_gqa, cv_gqa, ck_na, cv_na, st_dn)
    return tuple(np.ascontiguousarray(o, dtype=np.float32) for o in outs)
```

```python
import contextlib
import numpy as np
import concourse.bass as bass
import concourse.mybir as mybir
from concourse.bass_utils import run_bass_kernel_spmd

F32 = mybir.dt.float32
BF16 = mybir.dt.bfloat16
AF = mybir.ActivationFunctionType
ALU = mybir.AluOpType
AX = mybir.AxisListType
PE, ACT, DVE, POOL, SP = "tensor", "scalar", "vector", "gpsimd", "sync"

NCORES = 8
D = 2048
DFF = 8192
NP_TOK = 512
NS_TOK = 2048
PAST = 256
EV_COLS = 4640
OD_COLS = 7200
EPS = 1e-6
DEBUG = False
UPTO = 99


class Trk:
    __slots__ = ("last_w", "readers")

    def __init__(self):
        self.last_w = None
        self.readers = []


class Op:
    __slots__ = ("eng", "fn", "deps", "signal", "sig", "is_dma", "lane")

    def __init__(self, eng, fn, is_dma, lane):
        self.eng = eng
        self.fn = fn
        self.deps = []
        self.signal = False
        self.sig = None
        self.is_dma = is_dma
        self.lane = lane


class Prog:
    def __init__(self, nc):
        self.nc = nc
        self.ops = []
        self.stack = contextlib.ExitStack()
        self.lane_last = {}
        self.trks = []

    def trk(self):
        t = Trk()
        self.trks.append(t)
        return t

    def op(self, eng, fn, R=(), W=(), is_dma=False, lane=None):
        o = Op(eng, fn, is_dma, lane)
        deps = []
        for r in R:
            if r.last_w is not None:
                deps.append(r.last_w)
        for w in W:
            if w.last_w is not None:
                deps.append(w.last_w)
            deps.extend(w.readers)
        if is_dma:
            prev = self.lane_last.get(lane)
            if prev is not None:
                deps.append(prev)
            self.lane_last[lane] = o
        seen = set()
        for d in deps:
            if d is o or id(d) in seen:
                continue
            seen.add(id(d))
            if (not d.is_dma) and (not is_dma) and d.eng == PE and eng == PE:
                continue
            o.deps.append(d)
            d.signal = True
        for r in R:
            r.readers.append(o)
        for w in W:
            w.last_w = o
            w.readers = []
        self.ops.append(o)
        return o

    def barrier(self):
        allt = list(self.trks)
        deps = []
        seen = set()
        for t in allt:
            for d in ([t.last_w] if t.last_w is not None else []) + t.readers:
                if id(d) not in seen:
                    seen.add(id(d))
                    deps.append(d)
        for l, d in self.lane_last.items():
            if id(d) not in seen:
                seen.add(id(d))
                deps.append(d)
        for e in (PE, ACT, DVE, POOL, SP):
            o = Op(e, None, False, None)
            for d in deps:
                if d.fn is None:
                    continue
                o.deps.append(d)
                d.signal = True
            self.ops.append(o)
        for t in allt:
            t.last_w = None
            t.readers = []

    def dma(self, q, out, in_, R, W, lane):
        return self.op(q, lambda e: e.dma_start(out=out, in_=in_), R, W, True, q + ":" + lane)

    def mm(self, out, lhsT, rhs, start, stop, R, W):
        return self.op(PE, lambda e: e.matmul(out, lhsT=lhsT, rhs=rhs, start=start, stop=stop), R, W)

    def tr(self, out, in_, ident, R, W):
        return self.op(PE, lambda e: e.transpose(out=out, in_=in_, identity=ident), R, W)

    def act(self, out, in_, func, R, W, **kw):
        return self.op(ACT, lambda e: e.activation(out=out, in_=in_, func=func, **kw), R, W)

    def tt(self, eng, out, in0, in1, op, R, W):
        return self.op(eng, lambda e: e.tensor_tensor(out=out, in0=in0, in1=in1, op=op), R, W)

    def ts(self, eng, out, in0, s1, s2, op0, op1, R, W):
        if s2 is None:
            return self.op(eng, lambda e: e.tensor_scalar(out=out, in0=in0, scalar1=s1, scalar2=None, op0=op0), R, W)
        return self.op(eng, lambda e: e.tensor_scalar(out=out, in0=in0, scalar1=s1, scalar2=s2, op0=op0, op1=op1), R, W)

    def stt(self, eng, out, in0, scalar, in1, op0, op1, R, W):
        return self.op(eng, lambda e: e.scalar_tensor_tensor(out=out, in0=in0, scalar=scalar, in1=in1, op0=op0, op1=op1), R, W)

    def cp(self, eng, out, in_, R, W):
        if eng == ACT:
            return self.op(ACT, lambda e: e.copy(out=out, in_=in_), R, W)
        return self.op(eng, lambda e: e.tensor_copy(out=out, in_=in_), R, W)

    def red(self, out, in_, op, R, W):
        return self.op(DVE, lambda e: e.tensor_reduce(out=out, in_=in_, axis=AX.X, op=op), R, W)

    def memset(self, eng, ap, val, W):
        return self.op(eng, lambda e: e.memset(ap, val), (), W)

    def recip(self, out, in_, R, W):
        return self.op(DVE, lambda e: e.reciprocal(out=out, in_=in_), R, W)

    def emit(self):
        nc = self.nc
        engs = [PE, ACT, DVE, POOL, SP]
        esem = {e: self.stack.enter_context(nc.semaphore(f"s_{e}")) for e in engs}
        lanes = {}
        for o in self.ops:
            if o.is_dma and o.lane not in lanes:
                lanes[o.lane] = self.stack.enter_context(nc.semaphore(f"l_{len(lanes)}"))
        cnt = {e: 0 for e in engs}
        lcnt = {l: 0 for l in lanes}
        for o in self.ops:
            if o.fn is None:
                continue
            if o.is_dma:
                lcnt[o.lane] += 16
                o.sig = (lanes[o.lane], lcnt[o.lane])
                o.signal = True
            elif o.signal:
                cnt[o.eng] += 1
                o.sig = (esem[o.eng], cnt[o.eng])
        per = {e: [] for e in engs}
        for o in self.ops:
            per[o.eng].append(o)
        final_waits = [(lanes[l], lcnt[l]) for l in lanes]
        with nc.Block() as block:
            def body(ename):
                def run(eng):
                    waited = {}
                    for o in per[ename]:
                        need = {}
                        for d in o.deps:
                            if d.sig is None:
                                continue
                            s, v = d.sig
                            k = id(s)
                            if waited.get(k, 0) >= v:
                                continue
                            if k not in need or need[k][1] < v:
                                need[k] = (s, v)
                        for k, (s, v) in need.items():
                            eng.wait_ge(s, v)
                            waited[k] = v
                        if o.fn is None:
                            continue
                        ins = o.fn(eng)
                        if o.signal:
                            ins.then_inc(o.sig[0], 16 if o.is_dma else 1)
                    if ename == SP:
                        for s, v in final_waits:
                            eng.wait_ge(s, v)
                return run
            block.tensor(body(PE))
            block.scalar(body(ACT))
            block.vector(body(DVE))
            block.gpsimd(body(POOL))
            block.sync(body(SP))
        self.stack.close()


class Tile:
    __slots__ = ("ap", "t")

    def __init__(self, ap, t):
        self.ap = ap
        self.t = t


class Arena:
    def __init__(self, P, base_ap, nwords):
        self.P = P
        self.base = base_ap
        self.n = nwords
        self.off = 0
        self.static_off = 0

    def reset(self, mixer=False):
        self.off = self.mixer_off if mixer else self.static_off

    def freeze(self):
        self.static_off = self.off

    def f32(self, nw, shape=None):
        assert self.off + nw <= self.n, f"arena overflow {self.off}+{nw}>{self.n}"
        ap = self.base[:, self.off:self.off + nw]
        self.off += nw
        if shape:
            ap = _shape(ap, shape)
        return Tile(ap, self.P.trk())

    def bf16(self, n, shape=None):
        nw = (n + 1) // 2
        assert self.off + nw <= self.n, f"arena overflow {self.off}+{nw}>{self.n}"
        ap = self.base[:, self.off:self.off + nw].bitcast(BF16)
        self.off += nw
        if shape:
            ap = _shape(ap, shape)
        return Tile(ap, self.P.trk())


def _shape(ap, shape):
    if len(shape) == 2:
        return ap.rearrange("p (a b) -> p a b", a=shape[0], b=shape[1])
    if len(shape) == 3:
        return ap.rearrange("p (a b c) -> p a b c", a=shape[0], b=shape[1], c=shape[2])
    if len(shape) == 4:
        return ap.rearrange("p (a b c d) -> p a b c d", a=shape[0], b=shape[1], c=shape[2], d=shape[3])
    raise ValueError


def bc_mid(ap2, n):
    p, f = ap2.shape
    return ap2.unsqueeze(1).to_broadcast([p, n, f])


def bc_last(ap2, n):
    p, h = ap2.shape
    return ap2.unsqueeze(2).to_broadcast([p, h, n])


class K:
    pass


def dram_in(nc, name, shape, dt=F32):
    return nc.dram_tensor(name, list(shape), dt, kind="ExternalInput").ap()


def dram_out(nc, name, shape, dt=F32):
    return nc.dram_tensor(name, list(shape), dt, kind="ExternalOutput").ap()


def dram_scr(nc, name, shape, dt=F32):
    kind = "ExternalOutput" if DEBUG else "Internal"
    return nc.dram_tensor(name, list(shape), dt, kind=kind).ap()


INPUT_SHAPES = {
    "xp": (NP_TOK, D), "xs": (NS_TOK, D), "condT": (128, 32),
    "sgla": (2, 8, 64, 128), "cgk": (PAST, 256), "cgv": (PAST, 256), "cnk": (PAST, 1024), "cnv": (PAST, 1024),
    "sdl": (2, 8, 128, 128),
    "w_ada": (2, D, 6 * D), "b_ada": (2, 6 * D), "norm1": (2, D), "norm2": (2, D),
    "w_mlp1": (2, D, DFF), "w_mlp2": (2, DFF, D),
    "ev_w_in": (D, EV_COLS), "wa2": (2, 17, 512), "gla_norm": (1, 128), "qk_gain": (1, 1280),
    "ev_w_out": (D, D), "od_w_in": (D, OD_COLS), "od_conv": (3, 3072), "od_alog": (1, 16), "od_dtb": (1, 16),
    "dn_norm": (1, 128), "na_bias": (5, 128, 8 * 576), "na_mask": (5, 128, 576), "od_w_out": (D, D), "norm_f": (1, D),
    "ident": (128, 128), "tri": (11, 128, 128), "rope": (NS_TOK, 128),
}


def build_program():
    nc = bass.Bass("TRN2", target_bir_lowering=False)
    k = K()
    k.nc = nc
    k.i = {n: dram_in(nc, n, s) for n, s in INPUT_SHAPES.items()}
    k.o = {
        "yp": dram_out(nc, "yp", (NP_TOK, D)), "ys": dram_out(nc, "ys", (NS_TOK, D)),
        "nsg": dram_out(nc, "nsg", (2, 2, 8, 64, 128)),
        "ngk": dram_out(nc, "ngk", (NP_TOK, 256)), "ngv": dram_out(nc, "ngv", (NP_TOK, 256)),
        "nnk": dram_out(nc, "nnk", (NP_TOK, 1024)), "nnv": dram_out(nc, "nnv", (NP_TOK, 1024)),
        "nsd": dram_out(nc, "nsd", (2, 2, 8, 128, 128)),
    }
    k.mods = dram_scr(nc, "mods", (2, 2, 6 * D))
    k.wc_dram = nc.dram_tensor("wcache", [110, 128, 8192], BF16, kind="Internal").ap()
    k.wcache = {}
    P = Prog(nc)
    k.P = P
    sb = P.stack.enter_context(nc.sbuf_tensor("arena", [128, 52000], F32))
    k.ar = Arena(P, sb, 52000)
    ps = P.stack.enter_context(nc.psum_tensor("psum", [128, 4096], F32))
    k.ps = ps
    k.bank_t = [P.trk() for _ in range(8)]

    class G:
        pass
    gp, gs = G(), G()
    gp.name, gp.NT, gp.x_in, gp.row, gp.y = "p", NP_TOK, k.i["xp"], 0, k.o["yp"]
    gs.name, gs.NT, gs.x_in, gs.row, gs.y = "s", NS_TOK, k.i["xs"], 1, k.o["ys"]
    gp.seqs = [(0, 256, False, 0), (256, 256, False, 1)]
    gs.seqs = [(0, 2048, True, 0)]
    for g in (gp, gs):
        g.xres = dram_scr(nc, f"xres_{g.name}", (g.NT, D))
        g.zin = dram_scr(nc, f"zin_{g.name}", (g.NT, OD_COLS))
        g.nqkT = dram_scr(nc, f"nqkT_{g.name}", (2048, g.NT), BF16)
        g.mix = dram_scr(nc, f"mix_{g.name}", (g.NT, D), BF16)
        g.t_xres = [P.trk() for _ in range(g.NT // 512)]
        g.t_zin = P.trk()
        g.t_nqkT = P.trk()
        g.t_mix = P.trk()
        g.ofw = dram_scr(nc, f"ofw_{g.name}", (g.NT, 1024))
        g.t_ofw = P.trk()
        g.obw = dram_scr(nc, f"obw_{g.name}", (g.NT, 1024))
        g.t_obw = P.trk()
    k.groups = [gp, gs]
    k.t_mods = P.trk()

    setup_static(k)
    if UPTO >= 1:
        phase_ada(k)
    for l in range(2):
        if UPTO >= 2 + 4 * l:
            for g in k.groups:
                phase_A(k, l, g)
        if UPTO >= 3 + 4 * l:
            for g in k.groups:
                for sq in g.seqs:
                    if l == 0:
                        mixer_even(k, g, sq)
                    else:
                        mixer_odd(k, g, sq)
        if UPTO >= 4 + 4 * l:
            for g in k.groups:
                phase_C(k, l, g)
    P.emit()
    return nc


def bank(k, b, nb=1):
    return k.ps[:, b * 512:(b + nb) * 512]


def bank_bf(k, b):
    return k.ps[:, b * 512:(b + 1) * 512].bitcast(BF16)


def setup_static(k):
    P, ar = k.P, k.ar
    k.identf = ar.f32(128)
    k.identb = ar.bf16(128)
    k.ones_f = ar.f32(128)
    k.small = ar.f32(64)
    ar.mixer_off = ar.off
    k.wring = [ar.bf16(16 * 512, (16, 512)) for _ in range(3)]
    k.wr_i = 0
    P.dma(SP, k.identf.ap, k.i["ident"], [], [k.identf.t], "c0")
    P.cp(DVE, k.identb.ap, k.identf.ap, [k.identf.t], [k.identb.t])
    P.memset(DVE, k.ones_f.ap, 1.0, [k.ones_f.t])
    ar.freeze()


def wload(k, src, key=None):
    P = k.P
    i = k.wr_i
    k.wr_i = (i + 1) % 3
    wt = k.wring[i]
    n = src.shape[1]
    ent = k.wcache.get(key) if key is not None else None
    if ent is not None:
        idx, tk = ent
        P.dma(POOL, wt.ap[:, :, 0:n], k.wc_dram[idx, :, 0:16 * n].rearrange("p (c n) -> p c n", n=n), [tk], [wt.t], f"w{i}")
        return wt
    P.dma(POOL, wt.ap[:, :, 0:n], src.rearrange("(c p) n -> p c n", p=128), [], [wt.t], f"w{i}")
    if key is not None and len(k.wcache) < k.wc_dram.shape[0]:
        idx = len(k.wcache)
        tk = P.trk()
        k.wcache[key] = (idx, tk)
        P.dma(SP, k.wc_dram[idx, :, 0:16 * n].rearrange("p (c n) -> p c n", n=n), wt.ap[:, :, 0:n], [wt.t], [tk], f"wc{i}")
    return wt


def load_bc(k, dst, src_row, R=()):
    k.P.dma(SP, dst.ap, src_row.partition_broadcast(128), list(R), [dst.t], "bc")


def phase_ada(k):
    P, ar = k.P, k.ar
    ar.reset()
    P.barrier()
    cT = ar.f32(32)
    scT = ar.bf16(32, (16, 2))
    brow = [ar.f32(512) for _ in range(2)]
    orow = [ar.f32(512) for _ in range(2)]
    P.dma(SP, cT.ap, k.i["condT"], [], [cT.t], "c0")
    P.act(scT.ap.rearrange("p a b -> p (a b)"), cT.ap, AF.Silu, [cT.t], [scT.t])
    it = 0
    for l in range(2):
        for nb in range(24):
            wt = wload(k, k.i["w_ada"][l, :, nb * 512:(nb + 1) * 512])
            b = it % 4
            pb = bank(k, b)
            for c in range(16):
                P.mm(pb[0:2, :], scT.ap[:, c, :], wt.ap[:, c, :], c == 0, c == 15, [scT.t, wt.t], [k.bank_t[b]])
            br, orw = brow[it % 2], orow[it % 2]
            P.dma(SP, br.ap[0:2, :], k.i["b_ada"][l:l + 1, nb * 512:(nb + 1) * 512].partition_broadcast(2), [], [br.t], f"ab{it % 2}")
            P.tt(DVE, orw.ap[0:2, :], pb[0:2, :], br.ap[0:2, :], ALU.add, [k.bank_t[b], br.t], [orw.t])
            P.dma(SP, k.mods[l, :, nb * 512:(nb + 1) * 512], orw.ap[0:2, :], [orw.t], [k.t_mods], f"ao{it % 2}")
            it += 1


def mod_row(k, l, g, j):
    return k.mods[l, g.row:g.row + 1, j * D:(j + 1) * D]


def make_AB(k, l, g, jsc, jsh, normname, A, B, tmp):
    P = k.P
    load_bc(k, tmp, mod_row(k, l, g, jsc), [k.t_mods])
    load_bc(k, A, k.i[normname][l:l + 1, :])
    P.stt(DVE, A.ap, tmp.ap, 1.0, A.ap, ALU.add, ALU.mult, [tmp.t, A.t], [A.t])
    load_bc(k, B, mod_row(k, l, g, jsh), [k.t_mods])


def norm_mod_T(k, x_sub, A, B, tmp, hb, hT, s, tb0, tb1, plain=None):
    P = k.P
    if plain is None:
        ssq = k.small.ap[:, 0:1]
        rstd = k.small.ap[:, 1:2]
        P.act(tmp.ap, x_sub, AF.Square, [k.x_t], [tmp.t, k.small.t], accum_out=ssq)
        P.act(rstd, ssq, AF.Sqrt, [k.small.t], [k.small.t], scale=1.0 / D, bias=EPS)
        P.recip(rstd, rstd, [k.small.t], [k.small.t])
        P.stt(DVE, tmp.ap, x_sub, rstd, A.ap, ALU.mult, ALU.mult, [k.x_t, k.small.t, A.t], [tmp.t])
        P.tt(DVE, hb.ap, tmp.ap, B.ap, ALU.add, [tmp.t, B.t], [hb.t])
        src = hb.ap
        src_t = hb.t
    else:
        src, src_t = plain
    for half in range(2):
        b = tb0 if half == 0 else tb1
        pv = bank_bf(k, b).rearrange("p (a b) -> p a b", a=8)
        for c in range(8):
            cc = half * 8 + c
            P.tr(pv[:, c, :], src[:, cc * 128:(cc + 1) * 128], k.identb.ap, [src_t, k.identb.t], [k.bank_t[b]])
        eng = DVE if half == 0 else ACT
        P.cp(eng, hT.ap[:, half * 8:(half + 1) * 8, s * 128:(s + 1) * 128], pv, [k.bank_t[b]], [hT.t])


def phase_A(k, l, g):
    P, ar = k.P, k.ar
    ar.reset()
    P.barrier()
    A = ar.f32(D)
    B = ar.f32(D)
    tmp = ar.f32(D)
    xt = ar.f32(4 * D, (4, D))
    hb = [ar.bf16(D) for _ in range(2)]
    hT = ar.bf16(16 * 512, (16, 512))
    stg = [ar.f32(4 * 512, (4, 512)) for _ in range(2)]
    stgT = [ar.bf16(4 * 512, (4, 512)) for _ in range(2)]
    make_AB(k, l, g, 1, 0, "norm1", A, B, tmp)
    k.x_t = xt.t
    w_in = k.i["ev_w_in"] if l == 0 else k.i["od_w_in"]
    ncols = EV_COLS if l == 0 else OD_COLS
    fm_blocks = 0 if l == 0 else 4
    tm_start = 0 if l == 0 else 1024
    x_src = g.x_in if l == 0 else g.xres
    it = 0
    for tile in range(g.NT // 512):
        rows = slice(tile * 512, (tile + 1) * 512)
        R = [] if l == 0 else [g.t_xres[tile]]
        P.dma(SP, xt.ap, x_src[rows, :].rearrange("(s p) d -> p s d", p=128), R, [xt.t], "x")
        for s in range(4):
            norm_mod_T(k, xt.ap[:, s, :], A, B, tmp, hb[s % 2], hT, s, 0, 1)
        for fb in range(fm_blocks):
            wt = wload(k, w_in[:, fb * 512:(fb + 1) * 512], ("in", l, fb * 512))
            st = stgT[it % 2]
            for j in range(4):
                b = 2 + (it * 4 + j) % 6
                pb = bank(k, b)
                for c in range(16):
                    P.mm(pb, wt.ap[:, c, j * 128:(j + 1) * 128], hT.ap[:, c, :], c == 0, c == 15, [wt.t, hT.t], [k.bank_t[b]])
                P.cp(ACT if j % 2 == 0 else DVE, st.ap[:, j, :], pb, [k.bank_t[b]], [st.t])
            P.dma(SP, g.nqkT[fb * 512:(fb + 1) * 512, rows].rearrange("(j p) t -> p j t", p=128), st.ap, [st.t], [g.t_nqkT], f"sT{it % 2}")
            it += 1
        c0 = tm_start
        while c0 < ncols:
            n = min(512, ncols - c0)
            wt = wload(k, w_in[:, c0:c0 + n], ("in", l, c0))
            st = stg[it % 2]
            for s in range(4):
                b = 2 + (it * 4 + s) % 6
                pb = bank(k, b)
                for c in range(16):
                    P.mm(pb[:, 0:n], hT.ap[:, c, s * 128:(s + 1) * 128], wt.ap[:, c, 0:n], c == 0, c == 15, [wt.t, hT.t], [k.bank_t[b]])
                P.cp(ACT if s % 2 == 0 else DVE, st.ap[:, s, 0:n], pb[:, 0:n], [k.bank_t[b]], [st.t])
            P.dma(SP, g.zin[rows, c0:c0 + n].rearrange("(s p) n -> p s n", p=128), st.ap[:, :, 0:n], [st.t], [g.t_zin], f"st{it % 2}")
            it += 1
            c0 += n


def phase_C(k, l, g):
    P, ar = k.P, k.ar
    ar.reset()
    P.barrier()
    G1 = ar.f32(D)
    A = ar.f32(D)
    B = ar.f32(D)
    tmp = ar.f32(D)
    xt = ar.f32(4 * D, (4, D))
    mixb = ar.bf16(4 * D, (4, D))
    hT = ar.bf16(16 * 512, (16, 512))
    hid = ar.bf16(32 * 512, (32, 512))
    rl = [ar.f32(512) for _ in range(2)]
    k.x_t = xt.t
    make_AB(k, l, g, 4, 3, "norm2", A, B, tmp)
    w_out = k.i["ev_w_out"] if l == 0 else k.i["od_w_out"]
    x_src = g.x_in if l == 0 else g.xres
    it = 0
    for tile in range(g.NT // 512):
        rows = slice(tile * 512, (tile + 1) * 512)
        R = [] if l == 0 else [g.t_xres[tile]]
        P.dma(SP, xt.ap, x_src[rows, :].rearrange("(s p) d -> p s d", p=128), R, [xt.t], "x")
        P.dma(SP, mixb.ap, g.mix[rows, :].rearrange("(s p) d -> p s d", p=128), [g.t_mix], [mixb.t], "mx")
        load_bc(k, G1, mod_row(k, l, g, 2), [k.t_mods])
        for s in range(4):
            norm_mod_T(k, None, None, None, None, None, hT, s, 0, 1, plain=(mixb.ap[:, s, :], mixb.t))
        for nb in range(4):
            wt = wload(k, w_out[:, nb * 512:(nb + 1) * 512], ("out", l, nb))
            for s in range(4):
                b = 2 + (it % 6)
                it += 1
                pb = bank(k, b)
                for c in range(16):
                    P.mm(pb, hT.ap[:, c, s * 128:(s + 1) * 128], wt.ap[:, c, :], c == 0, c == 15, [wt.t, hT.t], [k.bank_t[b]])
                r = rl[it % 2]
                P.tt(DVE, r.ap, pb, G1.ap[:, nb * 512:(nb + 1) * 512], ALU.mult, [k.bank_t[b], G1.t], [r.t])
                P.tt(DVE, xt.ap[:, s, nb * 512:(nb + 1) * 512], xt.ap[:, s, nb * 512:(nb + 1) * 512], r.ap, ALU.add, [r.t, xt.t], [xt.t])
        for s in range(4):
            hbt = Tile(mixb.ap[:, s % 2, :], mixb.t)
            norm_mod_T(k, xt.ap[:, s, :], A, B, tmp, hbt, hT, s, 0, 1)
        load_bc(k, G1, mod_row(k, l, g, 5), [k.t_mods])
        for half in range(2):
            for fb in range(8):
                f0 = half * 4096 + fb * 512
                wt = wload(k, k.i["w_mlp1"][l, :, f0:f0 + 512], ("m1", l, f0))
                for j in range(4):
                    b = 2 + (it % 6)
                    it += 1
                    pb = bank(k, b)
                    for c in range(16):
                        P.mm(pb, wt.ap[:, c, j * 128:(j + 1) * 128], hT.ap[:, c, :], c == 0, c == 15, [wt.t, hT.t], [k.bank_t[b]])
                    r = rl[it % 2]
                    P.act(r.ap, pb, AF.Relu, [k.bank_t[b]], [r.t])
                    P.tt(DVE, hid.ap[:, fb * 4 + j, :], r.ap, r.ap, ALU.mult, [r.t], [hid.t])
            for nb in range(4):
                bs = [2, 3, 4, 5] if nb % 2 == 0 else [6, 7, 0, 1]
                for kq in range(2):
                    r0 = half * 4096 + kq * 2048
                    wt = wload(k, k.i["w_mlp2"][l, r0:r0 + 2048, nb * 512:(nb + 1) * 512], ("m2", l, r0, nb))
                    for c in range(16):
                        for s in range(4):
                            P.mm(bank(k, bs[s]), hid.ap[:, kq * 16 + c, s * 128:(s + 1) * 128], wt.ap[:, c, :],
                                 kq == 0 and c == 0, kq == 1 and c == 15, [wt.t, hid.t], [k.bank_t[bs[s]]])
                for s in range(4):
                    r = rl[s % 2]
                    P.tt(DVE, r.ap, bank(k, bs[s]), G1.ap[:, nb * 512:(nb + 1) * 512], ALU.mult, [k.bank_t[bs[s]], G1.t], [r.t])
                    P.tt(DVE, xt.ap[:, s, nb * 512:(nb + 1) * 512], xt.ap[:, s, nb * 512:(nb + 1) * 512], r.ap, ALU.add, [r.t, xt.t], [xt.t])
        if l == 0:
            P.dma(SP, g.xres[rows, :].rearrange("(s p) d -> p s d", p=128), xt.ap, [xt.t], [g.t_xres[tile]], "xo")
        else:
            load_bc(k, A, k.i["norm_f"][0:1, :])
            for s in range(4):
                ssq = k.small.ap[:, 0:1]
                rstd = k.small.ap[:, 1:2]
                P.act(tmp.ap, xt.ap[:, s, :], AF.Square, [xt.t], [tmp.t, k.small.t], accum_out=ssq)
                P.act(rstd, ssq, AF.Sqrt, [k.small.t], [k.small.t], scale=1.0 / D, bias=EPS)
                P.recip(rstd, rstd, [k.small.t], [k.small.t])
                P.stt(DVE, xt.ap[:, s, :], xt.ap[:, s, :], rstd, A.ap, ALU.mult, ALU.mult, [xt.t, k.small.t, A.t], [xt.t])
            P.dma(SP, g.y[rows, :].rearrange("(s p) d -> p s d", p=128), xt.ap, [xt.t], [], "xo")
            if tile + 1 < g.NT // 512:
                make_AB(k, l, g, 4, 3, "norm2", A, B, tmp)


def mixer_even(k, g, sq):
    gla_part(k, g, sq)
    gqa_part(k, g, sq)


def gla_part(k, g, sq):
    t0, T, is_s, si = sq
    P, ar = k.P, k.ar
    ar.reset(mixer=True)
    P.barrier()
    NTL = T // 128
    zin = g.zin
    tri = ar.f32(4 * 128, (4, 128))
    msk = ar.f32(2 * 128, (2, 128))
    wa2 = ar.f32(2 * 512, (2, 512))
    gn = ar.f32(128)
    gloT = ar.f32(128)
    S = ar.f32(1024, (8, 128))
    Sb = ar.bf16(1024, (8, 128))
    ofw = ar.f32(NTL * 1024, (NTL, 1024))
    qk = ar.f32(1024)
    vb = ar.bf16(1024, (8, 128))
    glo = ar.f32(32)
    e1 = ar.f32(512)
    sp = ar.f32(512)
    eb, enb, ekd = ar.f32(512), ar.f32(512), ar.f32(512)
    qe, ke, kd = ar.bf16(512), ar.bf16(512), ar.bf16(512)
    qeT = ar.bf16(1024, (8, 128))
    keT = ar.bf16(1024, (8, 128))
    AT = ar.bf16(1024, (8, 128))
    dec = ar.f32(8)
    tmpS = ar.f32(1024, (8, 128))
    gg = ar.f32(1024)
    ot = ar.f32(1024, (8, 128))
    sqb = ar.f32(1024, (8, 128))
    ssq = ar.f32(8)
    mo = ar.bf16(1024, (8, 128))
    P.dma(SP, tri.ap, k.i["tri"][0:4].rearrange("a p f -> p a f"), [], [tri.t], "c0")
    P.dma(SP, msk.ap, k.i["tri"][4:6].rearrange("a p f -> p a f"), [], [msk.t], "c1")
    P.dma(SP, wa2.ap[0:17], k.i["wa2"].rearrange("d r n -> r d n"), [], [wa2.t], "c2")
    load_bc(k, gn, k.i["gla_norm"][0:1, :])
    P.memset(DVE, gloT.ap[0:32, :], 1.0, [gloT.t])
    onec = k.ones_f.ap[:, 0:1]
    bt = k.bank_t
    at_v = bank(k, 3, 2).rearrange("p (h c) -> p h c", h=8)
    o_v = bank(k, 5, 2).rearrange("p (h c) -> p h c", h=8)
    kv_v = bank(k, 1, 2).rearrange("p (h c) -> p h c", h=8)
    trq = bank_bf(k, 1).rearrange("p (h c) -> p h c", h=8)
    trk_ = bank_bf(k, 2).rearrange("p (h c) -> p h c", h=8)
    for dr in range(2):
        if is_s:
            P.dma(SP, S.ap[0:64], k.i["sgla"][dr].rearrange("h d v -> d h v"), [], [S.t], "c3")
        else:
            P.memset(DVE, S.ap[0:64], 0.0, [S.t])
        P.cp(ACT, Sb.ap[0:64], S.ap[0:64], [S.t], [Sb.t])
        order = range(NTL) if dr == 0 else range(NTL - 1, -1, -1)
        for ti in order:
            rows = slice(t0 + ti * 128, t0 + (ti + 1) * 128)
            P.dma(SP, qk.ap, zin[rows, 0:1024], [g.t_zin], [qk.t], "ld0")
            P.dma(POOL, vb.ap.rearrange("p h c -> p (h c)"), zin[rows, 1024:2048], [g.t_zin], [vb.t], "ld1")
            P.dma(SP, glo.ap, zin[rows, 3072:3104], [g.t_zin], [glo.t], "ld2")
            P.tr(bank(k, 7)[0:16, 0:128], glo.ap[:, dr * 16:(dr + 1) * 16], k.identf.ap, [glo.t, k.identf.t], [bt[7]])
            P.cp(DVE, gloT.ap[0:16, :], bank(k, 7)[0:16, 0:128], [bt[7]], [gloT.t])
            P.mm(bank(k, 0), gloT.ap[0:17, :], wa2.ap[0:17, dr, :], True, True, [gloT.t, wa2.t], [bt[0]])
            P.act(e1.ap, bank(k, 0), AF.Exp, [bt[0]], [e1.t], scale=-1.0)
            P.act(sp.ap, e1.ap, AF.Ln, [e1.t], [sp.t], bias=1.0)
            P.mm(bank(k, 1), tri.ap[:, 2 * dr, :], sp.ap, True, True, [tri.t, sp.t], [bt[1]])
            P.mm(bank(k, 2), tri.ap[:, 2 * dr + 1, :], sp.ap, True, True, [tri.t, sp.t], [bt[2]])
            P.act(eb.ap, bank(k, 1), AF.Exp, [bt[1]], [eb.t], scale=-1.0 / 16)
            P.act(enb.ap, bank(k, 1), AF.Exp, [bt[1]], [enb.t], scale=1.0 / 16)
            P.act(ekd.ap, bank(k, 2), AF.Exp, [bt[2]], [ekd.t], scale=-1.0 / 16)
            P.stt(DVE, qe.ap, qk.ap[:, 0:512], 0.125, eb.ap, ALU.mult, ALU.mult, [qk.t, eb.t], [qe.t])
            P.tt(POOL, ke.ap, qk.ap[:, 512:1024], enb.ap, ALU.mult, [qk.t, enb.t], [ke.t])
            P.tt(POOL, kd.ap, qk.ap[:, 512:1024], ekd.ap, ALU.mult, [qk.t, ekd.t], [kd.t])
            for h in range(8):
                P.mm(bank(k, 0)[0:64, h:h + 1], sp.ap[:, h * 64:(h + 1) * 64], onec, True, True, [sp.t, k.ones_f.t], [bt[0]])
            P.act(dec.ap[0:64, :], bank(k, 0)[0:64, 0:8], AF.Exp, [bt[0]], [dec.t], scale=-1.0 / 16)
            P.tt(POOL, tmpS.ap[0:64], S.ap[0:64], bc_last(dec.ap[0:64, :], 128), ALU.mult, [S.t, dec.t], [tmpS.t])
            for h in range(8):
                P.tr(trq[0:64, h, :], qe.ap[:, h * 64:(h + 1) * 64], k.identb.ap, [qe.t, k.identb.t], [bt[1]])
            P.cp(DVE, qeT.ap[0:64], trq[0:64], [bt[1]], [qeT.t])
            for h in range(8):
                P.tr(trk_[0:64, h, :], ke.ap[:, h * 64:(h + 1) * 64], k.identb.ap, [ke.t, k.identb.t], [bt[2]])
            P.cp(ACT, keT.ap[0:64], trk_[0:64], [bt[2]], [keT.t])
            for h in range(8):
                P.mm(at_v[:, h, :], keT.ap[0:64, h, :], qeT.ap[0:64, h, :], True, True, [keT.t, qeT.t], [bt[3], bt[4]])
            P.tt(DVE, AT.ap, at_v, bc_mid(msk.ap[:, dr, :], 8), ALU.mult, [bt[3], bt[4], msk.t], [AT.t])
            for h in range(8):
                P.mm(o_v[:, h, :], AT.ap[:, h, :], vb.ap[:, h, :], True, False, [AT.t, vb.t], [bt[5], bt[6]])
                P.mm(o_v[:, h, :], qeT.ap[0:64, h, :], Sb.ap[0:64, h, :], False, True, [qeT.t, Sb.t], [bt[5], bt[6]])
            for h in range(8):
                P.mm(kv_v[0:64, h, :], kd.ap[:, h * 64:(h + 1) * 64], vb.ap[:, h, :], True, True, [kd.t, vb.t], [bt[1], bt[2]])
            P.tt(DVE, S.ap[0:64], tmpS.ap[0:64], kv_v[0:64], ALU.add, [tmpS.t, bt[1], bt[2]], [S.t])
            P.cp(ACT, Sb.ap[0:64], S.ap[0:64], [S.t], [Sb.t])
            if dr == 0:
                P.cp(ACT, ofw.ap[:, ti, :], bank(k, 5, 2), [bt[5], bt[6]], [ofw.t])
            else:
                P.tt(DVE, ot.ap, o_v, ofw.ap[:, ti, :].rearrange("p (h c) -> p h c", h=8), ALU.add, [bt[5], bt[6], ofw.t], [ot.t])
                head_norm_gate(k, ot, sqb, ssq, gn, gg, mo, zin[rows, 2048:3072], g, g.mix[rows, 0:1024])
        if not is_s:
            P.dma(SP, k.o["nsg"][si, dr].rearrange("h d v -> d h v"), S.ap[0:64], [S.t], [], "so")


def head_norm_gate(k, ot, sqb, ssq, gn, gg, mo, gate_src, g, mix_dst, lane=""):
    P = k.P
    P.act(sqb.ap, ot.ap, AF.Square, [ot.t], [sqb.t])
    P.red(ssq.ap, sqb.ap, ALU.add, [sqb.t], [ssq.t])
    P.act(ssq.ap, ssq.ap, AF.Sqrt, [ssq.t], [ssq.t], scale=1.0 / 128, bias=EPS)
    P.recip(ssq.ap, ssq.ap, [ssq.t], [ssq.t])
    P.tt(DVE, ot.ap, ot.ap, bc_last(ssq.ap, 128), ALU.mult, [ot.t, ssq.t], [ot.t])
    P.tt(DVE, ot.ap, ot.ap, bc_mid(gn.ap, 8), ALU.mult, [ot.t, gn.t], [ot.t])
    P.dma(SP, gg.ap, gate_src, [g.t_zin], [gg.t], "ld3" + lane)
    P.act(gg.ap, gg.ap, AF.Silu, [gg.t], [gg.t])
    P.tt(DVE, mo.ap, ot.ap, gg.ap.rearrange("p (h c) -> p h c", h=8), ALU.mult, [ot.t, gg.t], [mo.t])
    P.dma(SP, mix_dst, mo.ap.rearrange("p h c -> p (h c)"), [mo.t], [g.t_mix], "mo" + lane)


def attention(k, g, qT, kT, NK, V, vcol, nheads, kvmap, qt, scale, Pb, PT, st, mo):
    P = k.P
    bt = k.bank_t
    NB = (NK + 127) // 128
    S_ps = bank(k, 0, 5)
    nsb = (NK + 511) // 512
    for h in range(nheads):
        gq = kvmap(h)
        for kc in range(0, NK, 512):
            n = min(512, NK - kc)
            P.mm(S_ps[:, kc:kc + n], qT.ap[:, h, qt * 128:(qt + 1) * 128], kT.ap[:, gq, kc:kc + n], True, True,
                 [qT.t, kT.t], [bt[kc // 512]])
        sbt = [bt[i] for i in range(nsb)]
        P.op(DVE, lambda e, o_=st.ap[:, 0:1], i_=S_ps[:, 0:NK]: e.tensor_reduce(out=o_, in_=i_, axis=AX.X, op=ALU.max), sbt, [st.t])
        P.ts(DVE, st.ap[:, 1:2], st.ap[:, 0:1], -scale, None, ALU.mult, None, [st.t], [st.t])
        P.act(Pb.ap[:, 0:NK], S_ps[:, 0:NK], AF.Exp, sbt + [st.t], [Pb.t, st.t], scale=scale, bias=st.ap[:, 1:2], accum_out=st.ap[:, 2:3])
        P.recip(st.ap[:, 3:4], st.ap[:, 2:3], [st.t], [st.t])
        for kb0 in range(0, NB, 8):
            nb = min(8, NB - kb0)
            b = 5 + (kb0 // 8) % 2
            pv = bank_bf(k, b).rearrange("p (a c) -> p a c", a=8)
            for i in range(nb):
                kb = kb0 + i
                P.tr(pv[:, i, :], Pb.ap[:, kb * 128:(kb + 1) * 128], k.identb.ap, [Pb.t, k.identb.t], [bt[b]])
            P.cp(DVE if (kb0 // 8) % 2 == 0 else ACT, PT.ap[:, kb0:kb0 + nb, :], pv[:, 0:nb, :], [bt[b]], [PT.t])
        o_ps = bank(k, 7)[:, 0:128]
        for kb in range(NB):
            P.mm(o_ps, PT.ap[:, kb, :], V.ap[:, kb, vcol(gq)], kb == 0, kb == NB - 1, [PT.t, V.t], [bt[7]])
        P.op(ACT, lambda e, o_=mo.ap[:, h, :], i_=o_ps, m_=st.ap[:, 3:4]: e.mul(out=o_, in_=i_, mul=m_), [bt[7], st.t], [mo.t])


def gqa_part(k, g, sq):
    t0, T, is_s, si = sq
    P, ar = k.P, k.ar
    ar.reset(mixer=True)
    P.barrier()
    NTL = T // 128
    zin = g.zin
    off = PAST if is_s else 0
    NK = T + off
    NB = NK // 128
    bt = k.bank_t
    gain = ar.f32(1280, (10, 128))
    qT = ar.bf16(8 * T, (8, T))
    kT = ar.bf16(2 * NK, (2, NK))
    V = ar.bf16(NB * 258, (NB, 258))
    aqk = ar.f32(1280, (10, 128))
    sqb = ar.f32(1280, (10, 128))
    ss = ar.f32(16)
    xr = ar.bf16(1280, (10, 128))
    rp = ar.f32(128)
    t1 = ar.f32(640, (10, 2, 32))
    t2 = ar.f32(640, (10, 2, 32))
    QN = min(512, T)
    NQ = QN // 128
    PTb = [ar.bf16(QN) for _ in range(2)]
    mo = ar.bf16(NQ * 1024, (NQ, 8, 128))
    st = ar.f32(8)
    P.memset(DVE, V.ap, 1.0, [V.t])
    P.dma(SP, gain.ap.rearrange("p a c -> p (a c)"), k.i["qk_gain"][0:1, :].partition_broadcast(128), [], [gain.t], "bc")
    tr0 = bank_bf(k, 0).rearrange("p (a c) -> p a c", a=8)
    tr1 = bank_bf(k, 1).rearrange("p (a c) -> p a c", a=8)
    if is_s:
        for cb in range(2):
            ck = Tile(aqk.ap.rearrange("p a c -> p (a c)")[:, 0:256], aqk.t)
            P.dma(SP, ck.ap, k.i["cgk"][cb * 128:(cb + 1) * 128, :], [], [ck.t], "ld0")
            ckb = Tile(xr.ap.rearrange("p a c -> p (a c)")[:, 0:256], xr.t)
            P.cp(DVE, ckb.ap, ck.ap, [ck.t], [ckb.t])
            for gk in range(2):
                P.tr(tr1[:, gk, :], ckb.ap[:, gk * 128:(gk + 1) * 128], k.identb.ap, [ckb.t, k.identb.t], [bt[1]])
            P.cp(ACT, kT.ap[:, :, cb * 128:(cb + 1) * 128], tr1[:, 0:2, :], [bt[1]], [kT.t])
            P.dma(POOL, V.ap[:, cb, :].rearrange("p (g c) -> p g c", c=129)[:, :, 0:128], k.i["cgv"][cb * 128:(cb + 1) * 128, :].rearrange("p (g c) -> p g c", c=128), [], [V.t], "ld1")
    for ti in range(NTL):
        rows = slice(t0 + ti * 128, t0 + (ti + 1) * 128)
        aflat = aqk.ap.rearrange("p a c -> p (a c)")
        P.dma(SP, aflat, zin[rows, 3104:4384], [g.t_zin], [aqk.t], "ld0")
        P.dma(POOL, V.ap[:, off // 128 + ti, :].rearrange("p (g c) -> p g c", c=129)[:, :, 0:128], zin[rows, 4384:4640].rearrange("p (g c) -> p g c", c=128), [g.t_zin], [V.t], "ld1")
        P.act(sqb.ap, aqk.ap, AF.Square, [aqk.t], [sqb.t])
        P.red(ss.ap[:, 0:10], sqb.ap, ALU.add, [sqb.t], [ss.t])
        P.act(ss.ap[:, 0:10], ss.ap[:, 0:10], AF.Sqrt, [ss.t], [ss.t], scale=1.0 / 128, bias=EPS)
        P.recip(ss.ap[:, 0:10], ss.ap[:, 0:10], [ss.t], [ss.t])
        P.tt(DVE, aqk.ap, aqk.ap, bc_last(ss.ap[:, 0:10], 128), ALU.mult, [aqk.t, ss.t], [aqk.t])
        P.tt(POOL, aqk.ap, aqk.ap, gain.ap, ALU.mult, [aqk.t, gain.t], [aqk.t])
        if not is_s:
            P.dma(SP, k.o["ngk"][rows, :], aflat[:, 1024:1280], [aqk.t], [], "so")
            P.dma(SP, k.o["ngv"][rows, :], zin[rows, 4384:4640], [g.t_zin], [], "so2")
            P.cp(ACT, xr.ap, aqk.ap, [aqk.t], [xr.t])
        else:
            P.dma(SP, rp.ap, k.i["rope"][ti * 128:(ti + 1) * 128, :], [], [rp.t], "ld2")
            xv = aqk.ap.rearrange("p a (i j f) -> p a i j f", i=2, j=2, f=32)
            ov = xr.ap.rearrange("p a (i j f) -> p a i j f", i=2, j=2, f=32)
            x1, x2 = xv[:, :, :, 0, :], xv[:, :, :, 1, :]
            cs = rp.ap[:, 0:64].rearrange("p (i f) -> p i f", i=2).unsqueeze(1).to_broadcast([128, 10, 2, 32])
            sn = rp.ap[:, 64:128].rearrange("p (i f) -> p i f", i=2).unsqueeze(1).to_broadcast([128, 10, 2, 32])
            P.tt(DVE, t1.ap, x1, cs, ALU.mult, [aqk.t, rp.t], [t1.t])
            P.tt(POOL, t2.ap, x2, sn, ALU.mult, [aqk.t, rp.t], [t2.t])
            P.tt(DVE, ov[:, :, :, 0, :], t1.ap, t2.ap, ALU.subtract, [t1.t, t2.t], [xr.t])
            P.tt(DVE, t1.ap, x2, cs, ALU.mult, [aqk.t, rp.t], [t1.t])
            P.tt(POOL, t2.ap, x1, sn, ALU.mult, [aqk.t, rp.t], [t2.t])
            P.tt(DVE, ov[:, :, :, 1, :], t1.ap, t2.ap, ALU.add, [t1.t, t2.t], [xr.t])
        for hh in range(8):
            P.tr(tr0[:, hh, :], xr.ap[:, hh, :], k.identb.ap, [xr.t, k.identb.t], [bt[0]])
        for hh in range(2):
            P.tr(tr1[:, hh, :], xr.ap[:, 8 + hh, :], k.identb.ap, [xr.t, k.identb.t], [bt[1]])
        P.cp(DVE, qT.ap[:, :, ti * 128:(ti + 1) * 128], tr0, [bt[0]], [qT.t])
        P.cp(ACT, kT.ap[:, :, off + ti * 128:off + (ti + 1) * 128], tr1[:, 0:2, :], [bt[1]], [kT.t])
    scale = 128 ** -0.5
    it = 0
    PTb = PTb + [ar.bf16(QN) for _ in range(2)]
    for qg in range(T // QN):
        q0 = qg * QN
        for h in range(8):
            gq = h // 4
            def s_mm(kb_, slot):
                P.mm(bank(k, slot)[:, 0:QN], kT.ap[:, gq, kb_ * 128:(kb_ + 1) * 128], qT.ap[:, h, q0:q0 + QN], True, True,
                     [kT.t, qT.t], [bt[slot]])
            base = it
            s_mm(0, base % 4)
            if NB > 1:
                s_mm(1, (base + 1) % 4)
            for kb in range(NB):
                sb_ = (base + kb) % 4
                if kb + 2 < NB:
                    s_mm(kb + 2, (base + kb + 2) % 4)
                sT = bank(k, sb_)[:, 0:QN]
                pt = PTb[sb_]
                P.act(pt.ap, sT, AF.Exp, [bt[sb_]], [pt.t], scale=scale)
                for j in range(NQ):
                    P.mm(bank(k, 4 + j)[:, 0:129], pt.ap[:, j * 128:(j + 1) * 128], V.ap[:, kb, gq * 129:(gq + 1) * 129], kb == 0, kb == NB - 1,
                         [pt.t, V.t], [bt[4 + j]])
            it = base + NB
            for j in range(NQ):
                ops = bank(k, 4 + j)[:, 0:129]
                P.recip(st.ap[:, j:j + 1], ops[:, 128:129], [bt[4 + j]], [st.t])
                P.op(ACT, lambda e, o_=mo.ap[:, j, h, :], i_=ops[:, 0:128], m_=st.ap[:, j:j + 1]: e.mul(out=o_, in_=i_, mul=m_),
                     [bt[4 + j], st.t], [mo.t])
        rows = slice(t0 + q0, t0 + q0 + QN)
        P.dma(SP, g.mix[rows, 1024:2048].rearrange("(j p) c -> p j c", p=128), mo.ap.rearrange("p j h c -> p j (h c)"), [mo.t], [g.t_mix], "mo")


def mixer_odd(k, g, sq):
    na_part(k, g, sq)
    dn_part(k, g, sq)


def na_part(k, g, sq):
    t0, T, is_s, si = sq
    P, ar = k.P, k.ar
    ar.reset(mixer=True)
    P.barrier()
    NTL = T // 128
    zin = g.zin
    bt = k.bank_t
    off = PAST if is_s else 0
    NK = off + T
    scale = 128 ** -0.5
    qT = ar.bf16(8 * T, (8, T))
    kT = ar.bf16(8 * NK, (8, NK))
    mo = ar.bf16(1024, (8, 128))
    st = ar.f32(8)
    P.dma(SP, qT.ap, g.nqkT[0:1024, t0:t0 + T].rearrange("(h p) t -> p h t", p=128), [g.t_nqkT], [qT.t], "ld0")
    P.dma(SP, kT.ap[:, :, off:off + T], g.nqkT[1024:2048, t0:t0 + T].rearrange("(h p) t -> p h t", p=128), [g.t_nqkT], [kT.t], "ld1")
    if not is_s:
        Pb = ar.bf16(256)
        PT = ar.bf16(2 * 128, (2, 128))
        V = ar.bf16(2 * 1024, (2, 1024))
        rows_all = slice(t0, t0 + T)
        P.dma(POOL, V.ap, zin[rows_all, 2048:3072].rearrange("(b p) n -> p b n", p=128), [g.t_zin], [V.t], "ld2")
        P.dma(SP, k.o["nnk"][rows_all, :], zin[rows_all, 1024:2048], [g.t_zin], [], "so")
        P.dma(SP, k.o["nnv"][rows_all, :], zin[rows_all, 2048:3072], [g.t_zin], [], "so2")
        for qt in range(NTL):
            rows = slice(t0 + qt * 128, t0 + (qt + 1) * 128)
            attention(k, g, qT, kT, NK, V, lambda h: slice(h * 128, (h + 1) * 128), 8, lambda h: h, qt, scale, Pb, PT, st, mo)
            P.dma(SP, g.mix[rows, 0:1024], mo.ap.rearrange("p h c -> p (h c)"), [mo.t], [g.t_mix], "mo")
        return
    ckf = ar.f32(1024, (8, 128))
    ckb = ar.bf16(1024, (8, 128))
    Vc = ar.bf16(2 * 1024, (2, 1024))
    vband = ar.bf16(5 * 1024, (5, 1024))
    bias = ar.f32(8 * 576, (8, 576))
    mask = ar.f32(576)
    NB2 = [dict(Ssb=ar.f32(832), Pb=ar.bf16(832), PT=ar.bf16(7 * 128, (7, 128)), st=ar.f32(8)) for _ in range(2)]
    tr5 = bank_bf(k, 5).rearrange("p (a c) -> p a c", a=8)
    for cb in range(2):
        P.dma(SP, ckf.ap.rearrange("p h c -> p (h c)"), k.i["cnk"][cb * 128:(cb + 1) * 128, :], [], [ckf.t], "ld2")
        P.cp(DVE, ckb.ap, ckf.ap, [ckf.t], [ckb.t])
        for h in range(8):
            P.tr(tr5[:, h, :], ckb.ap[:, h, :], k.identb.ap, [ckb.t, k.identb.t], [bt[5]])
        P.cp(ACT, kT.ap[:, :, cb * 128:(cb + 1) * 128], tr5, [bt[5]], [kT.t])
    P.dma(POOL, Vc.ap, k.i["cnv"].rearrange("(b p) n -> p b n", p=128), [], [Vc.t], "ld3")
    segs = [(0, 128), (128, 128), (256, 128), (384, 128), (512, 64), (576, 128), (704, 128)]
    cur_pat = -1
    S_ps = bank(k, 0, 2)
    for j in range(NTL):
        rows = slice(t0 + j * 128, t0 + (j + 1) * 128)
        pat, kb = na_pat(j), na_kb(j)
        b0 = t0 + kb * 64
        P.dma(POOL, vband.ap[:, 0:4, :], zin[b0:b0 + 512, 2048:3072].rearrange("(b p) n -> p b n", p=128), [g.t_zin], [vband.t], "ld4")
        P.dma(POOL, vband.ap[0:64, 4, :], zin[b0 + 512:b0 + 576, 2048:3072], [g.t_zin], [vband.t], "ld5")
        if pat != cur_pat:
            cur_pat = pat
            P.dma(SP, bias.ap.rearrange("p h c -> p (h c)"), k.i["na_bias"][pat], [], [bias.t], "ld6")
            P.dma(SP, mask.ap, k.i["na_mask"][pat], [], [mask.t], "ld7")
            P.tt(DVE, bias.ap, bias.ap, bc_mid(mask.ap, 8), ALU.add, [bias.t, mask.t], [bias.t])
        kofs = off + kb * 64

        def stage_a1(h):
            p = h % 2
            B = NB2[p]
            S_ps = bank(k, 2 * p, 2)
            bA, bB = bt[2 * p], bt[2 * p + 1]
            qs = qT.ap[:, h, j * 128:(j + 1) * 128]
            P.mm(S_ps[:, 0:512], qs, kT.ap[:, h, kofs:kofs + 512], True, True, [qT.t, kT.t], [bA])
            P.mm(S_ps[:, 512:576], qs, kT.ap[:, h, kofs + 512:kofs + 576], True, True, [qT.t, kT.t], [bB])
            P.mm(S_ps[:, 576:832], qs, kT.ap[:, h, 0:256], True, True, [qT.t, kT.t], [bB])
            Ssb, Pb, st = B["Ssb"], B["Pb"], B["st"]
            P.stt(DVE, Ssb.ap[:, 0:576], S_ps[:, 0:576], scale, bias.ap[:, h, :], ALU.mult, ALU.add, [bA, bB, bias.t], [Ssb.t])
            P.op(ACT, lambda e, o_=Ssb.ap[:, 576:832], i_=S_ps[:, 576:832]: e.mul(out=o_, in_=i_, mul=scale), [bB], [Ssb.t])

        def stage_a2(h):
            p = h % 2
            B = NB2[p]
            Ssb, Pb, st = B["Ssb"], B["Pb"], B["st"]
            P.op(DVE, lambda e, o_=st.ap[:, 0:1], i_=Ssb.ap: e.tensor_reduce(out=o_, in_=i_, axis=AX.X, op=ALU.max), [Ssb.t], [st.t])
            P.ts(DVE, st.ap[:, 1:2], st.ap[:, 0:1], -1.0, None, ALU.mult, None, [st.t], [st.t])
            P.act(Pb.ap, Ssb.ap, AF.Exp, [Ssb.t, st.t], [Pb.t, st.t], bias=st.ap[:, 1:2], accum_out=st.ap[:, 2:3])
            P.recip(st.ap[:, 3:4], st.ap[:, 2:3], [st.t], [st.t])


        def stage_b1(h):
            p = h % 2
            B = NB2[p]
            Pb, PT, st = B["Pb"], B["PT"], B["st"]
            tb = 5 + p
            trv = bank_bf(k, tb).rearrange("p (a c) -> p a c", a=8)
            for i, (c0, n) in enumerate(segs):
                P.tr(trv[0:n, i, :], Pb.ap[:, c0:c0 + n], k.identb.ap, [Pb.t, k.identb.t], [bt[tb]])
            P.cp(DVE if p == 0 else ACT, PT.ap, trv[:, 0:7, :], [bt[tb]], [PT.t])

        def stage_b2(h):
            p = h % 2
            B = NB2[p]
            Pb, PT, st = B["Pb"], B["PT"], B["st"]
            ob = 7 if p == 0 else 4
            o_ps = bank(k, ob)[:, 0:128]
            for i, (c0, n) in enumerate(segs):
                if i < 5:
                    rhs = vband.ap[0:n, i, h * 128:(h + 1) * 128]
                    rt = vband.t
                else:
                    rhs = Vc.ap[:, i - 5, h * 128:(h + 1) * 128]
                    rt = Vc.t
                P.mm(o_ps, PT.ap[0:n, i, :], rhs, i == 0, i == 6, [PT.t, rt], [bt[ob]])
            P.op(ACT, lambda e, o_=mo.ap[:, h, :], i_=o_ps, m_=st.ap[:, 3:4]: e.mul(out=o_, in_=i_, mul=m_), [bt[ob], st.t], [mo.t])


        stage_a1(0)
        stage_a2(0)
        for h in range(8):
            if h + 1 < 8:
                stage_a1(h + 1)
            stage_b1(h)
            if h + 1 < 8:
                stage_a2(h + 1)
            stage_b2(h)
        P.dma(SP, g.mix[rows, 0:1024], mo.ap.rearrange("p h c -> p (h c)"), [mo.t], [g.t_mix], "mo")


def dn_part(k, g, sq):
    t0, T, is_s, si = sq
    P, ar = k.P, k.ar
    ar.reset(mixer=True)
    P.barrier()
    NTL = T // 128
    C = K()
    C.tri = ar.f32(4 * 128, (4, 128))
    C.mks = ar.f32(4 * 128, (4, 128))
    C.bm = ar.f32(3 * 128, (3, 128))
    C.negA = ar.f32(16)
    C.dtb = ar.f32(16)
    P.dma(SP, C.tri.ap, k.i["tri"][0:4].rearrange("a p f -> p a f"), [], [C.tri.t], "c0")
    P.dma(SP, C.mks.ap, k.i["tri"][4:8].rearrange("a p f -> p a f"), [], [C.mks.t], "c1")
    P.dma(SP, C.bm.ap, k.i["tri"][8:11].rearrange("a p f -> p a f"), [], [C.bm.t], "c2")
    load_bc(k, C.dtb, k.i["od_dtb"][0:1, :])
    load_bc(k, C.negA, k.i["od_alog"][0:1, :])
    P.act(C.negA.ap, C.negA.ap, AF.Exp, [C.negA.t], [C.negA.t])
    P.ts(DVE, C.negA.ap, C.negA.ap, -1.0, None, ALU.mult, None, [C.negA.t], [C.negA.t])
    gens = [dn_chain(k, g, sq, dr, C) for dr in range(2)]
    lead = 34 if NTL > 2 else 20
    alive = [True, True]
    step = 0
    while any(alive):
        for ci in range(2):
            if not alive[ci]:
                continue
            if ci == 1 and step < lead and alive[0]:
                continue
            try:
                next(gens[ci])
            except StopIteration:
                alive[ci] = False
        step += 1
    ar.reset(mixer=True)
    P.barrier()
    gn = ar.f32(128)
    load_bc(k, gn, k.i["dn_norm"][0:1, :])
    bufs = []
    for _ in range(2):
        bufs.append(dict(a=ar.f32(1024, (8, 128)), b=ar.f32(1024), sq=ar.f32(1024, (8, 128)), ss=ar.f32(8), gg=ar.f32(1024), mo=ar.bf16(1024, (8, 128))))
    for ti in range(NTL):
        rows = slice(t0 + ti * 128, t0 + (ti + 1) * 128)
        B = bufs[ti % 2]
        P.dma(SP, B["a"].ap.rearrange("p h c -> p (h c)"), g.ofw[rows, :], [g.t_ofw], [B["a"].t], f"ca{ti % 2}")
        P.dma(SP, B["b"].ap, g.obw[rows, :], [g.t_obw], [B["b"].t], f"cb{ti % 2}")
        P.tt(DVE, B["a"].ap, B["a"].ap, B["b"].ap.rearrange("p (h c) -> p h c", h=8), ALU.add, [B["a"].t, B["b"].t], [B["a"].t])
        head_norm_gate(k, B["a"], B["sq"], B["ss"], gn, B["gg"], B["mo"], g.zin[rows, 6144:7168], g, g.mix[rows, 1024:2048], lane=f"{ti % 2}")


def dn_chain(k, g, sq, dr, C):
    t0, T, is_s, si = sq
    P, ar = k.P, k.ar
    NTL = T // 128
    zin = g.zin
    bt = k.bank_t
    L = f"d{dr}"
    tri, mks, bm, negA, dtb = C.tri, C.mks, C.bm, C.negA, C.dtb
    wc = ar.f32(3 * 1024, (3, 1024))
    X0 = ar.f32(1024, (8, 128))
    Xm = ar.f32(1024, (8, 128))
    Xp = ar.f32(1024, (8, 128))
    yv = ar.f32(1024, (8, 128))
    tv = ar.f32(1024, (8, 128))
    ssq = ar.f32(8)
    dab = ar.f32(32)
    gs = ar.f32(96)
    Ls = ar.f32(1024, (8, 128))
    LT = ar.f32(1024, (8, 128))
    knb = ar.bf16(1024, (8, 128))
    qnb = ar.bf16(1024, (8, 128))
    qeb = ar.bf16(1024, (8, 128))
    kdb = ar.bf16(1024, (8, 128))
    kT = ar.bf16(1024, (8, 128))
    qT = ar.bf16(1024, (8, 128))
    qeT = ar.bf16(1024, (8, 128))
    Nf = ar.f32(1024, (8, 128))
    NTf = ar.f32(1024, (8, 128))
    mk = lambda: ar.f32(512, (4, 128))
    sb = dict(P=mk(), PT=mk(), T=mk(), TT=mk(), A=mk(), B=mk(), No=mk(), NoT=mk())
    Y = ar.f32(2048, (8, 256))
    S = ar.f32(1024, (8, 128))
    Sb = ar.bf16(1024, (8, 128))
    tmpS = ar.f32(1024, (8, 128))
    fl = lambda t_: t_.ap.rearrange("p h c -> p (h c)")
    x0b = fl(X0).bitcast(BF16)
    yvb = fl(yv).bitcast(BF16)
    nkc = Tile(x0b[:, 0:1024].rearrange("p (h c) -> p h c", h=8), X0.t)
    nkcT = Tile(x0b[:, 1024:2048].rearrange("p (h c) -> p h c", h=8), X0.t)
    ub = Tile(yvb[:, 0:1024].rearrange("p (h c) -> p h c", h=8), yv.t)
    attnT = Tile(yvb[:, 1024:2048].rearrange("p (h c) -> p h c", h=8), yv.t)
    dg, D1, et, ot = Xm, Xp, tv, Xm
    beta, gt, gc, ngc, eg, egl, decb, nbeta, beg = [gs.ap[:, i * 8:(i + 1) * 8] for i in range(9)]
    v8 = lambda b: bank(k, b, 2).rearrange("p (h c) -> p h c", h=8)
    v4 = lambda b: bank(k, b).rearrange("p (h c) -> p h c", h=4)
    trb = lambda b: bank_bf(k, b).rearrange("p (a c) -> p a c", a=8)
    ng_v = v8(1)
    b0 = bank(k, 0)
    mLs = mks.ap[:, 3 if dr == 0 else 2, :]
    mLT = mks.ap[:, 0 if dr == 0 else 1, :]
    o_dst, t_o = (g.ofw, g.t_ofw) if dr == 0 else (g.obw, g.t_obw)
    if is_s:
        P.dma(SP, S.ap, k.i["sdl"][dr].rearrange("h d v -> d h v"), [], [S.t], L + "s")
    else:
        P.memset(DVE, S.ap, 0.0, [S.t])
    P.cp(ACT, Sb.ap, S.ap, [S.t], [Sb.t])
    order = range(NTL) if dr == 0 else range(NTL - 1, -1, -1)
    for ti in order:
        r0 = t0 + ti * 128
        rows = slice(r0, r0 + 128)
        P.dma(SP, dab.ap, zin[rows, 7168:7200], [g.t_zin], [dab.t], L + "g")
        P.act(beta, dab.ap[:, 16 + dr * 8:24 + dr * 8], AF.Sigmoid, [dab.t], [gs.t])
        P.tt(DVE, gt, dab.ap[:, dr * 8:dr * 8 + 8], dtb.ap[:, dr * 8:dr * 8 + 8], ALU.add, [dab.t, dtb.t], [gs.t])
        P.act(gt, gt, AF.Exp, [gs.t], [gs.t])
        P.act(gt, gt, AF.Ln, [gs.t], [gs.t], bias=1.0)
        P.tt(DVE, gt, gt, negA.ap[:, dr * 8:dr * 8 + 8], ALU.mult, [gs.t, negA.t], [gs.t])
        P.mm(b0[:, 0:8], tri.ap[:, 2 * dr, :], gt, True, True, [tri.t, gs.t], [bt[0]])
        P.mm(b0[:, 8:16], tri.ap[:, 2 * dr + 1, :], gt, True, True, [tri.t, gs.t], [bt[0]])
        P.mm(b0[:, 16:24], k.ones_f.ap, gt, True, True, [k.ones_f.t, gs.t], [bt[0]])
        P.cp(DVE, gc, b0[:, 0:8], [bt[0]], [gs.t])
        P.act(eg, b0[:, 0:8], AF.Exp, [bt[0]], [gs.t])
        P.act(egl, b0[:, 8:16], AF.Exp, [bt[0]], [gs.t])
        P.act(decb, b0[:, 16:24], AF.Exp, [bt[0]], [gs.t])
        P.ts(DVE, ngc, gc, -1.0, None, ALU.mult, None, [gs.t], [gs.t])
        P.ts(DVE, nbeta, beta, -1.0, None, ALU.mult, None, [gs.t], [gs.t])
        P.tt(DVE, beg, beta, eg, ALU.mult, [gs.t], [gs.t])
        P.tt(POOL, tmpS.ap, S.ap, bc_last(decb, 128), ALU.mult, [S.t, gs.t], [tmpS.t])
        yield
        P.tt(POOL, dg.ap, bc_mid(k.identf.ap, 8), bc_last(ngc, 128), ALU.mult, [k.identf.t, gs.t], [dg.t])
        dgf = fl(dg)
        P.mm(bank(k, 1), k.ones_f.ap, dgf[:, 0:512], True, True, [k.ones_f.t, dg.t], [bt[1]])
        P.mm(bank(k, 2), k.ones_f.ap, dgf[:, 512:1024], True, True, [k.ones_f.t, dg.t], [bt[2]])
        P.tt(DVE, D1.ap, ng_v, bc_last(gc, 128), ALU.add, [bt[1], bt[2], gs.t], [D1.t])
        yield
        P.ts(DVE, et.ap, D1.ap, 0.0, None, ALU.min, None, [D1.t], [et.t])
        P.act(et.ap, et.ap, AF.Exp, [et.t], [et.t])
        P.tt(POOL, Ls.ap, et.ap, bc_mid(mLs, 8), ALU.mult, [et.t, mks.t], [Ls.t])
        yield
        P.ts(DVE, et.ap, D1.ap, 0.0, None, ALU.max, None, [D1.t], [et.t])
        P.act(et.ap, et.ap, AF.Exp, [et.t], [et.t], scale=-1.0)
        P.tt(POOL, LT.ap, et.ap, bc_mid(mLT, 8), ALU.mult, [et.t, mks.t], [LT.t])
        yield
        for gi in range(3):
            c0 = 3072 + gi * 1024
            cols = slice(c0, c0 + 1024)
            P.dma(SP, wc.ap, k.i["od_conv"][:, gi * 1024:(gi + 1) * 1024].rearrange("(o a) n -> o a n", o=1).partition_broadcast(128), [], [wc.t], L + "w")
            P.dma(SP, fl(X0), zin[rows, cols], [g.t_zin], [X0.t], L + "0")
            if ti == 0:
                P.memset(DVE, fl(Xm), 0.0, [Xm.t])
                P.dma(SP, fl(Xm)[1:128, :], zin[r0:r0 + 127, cols], [g.t_zin], [Xm.t], L + "1")
            else:
                P.dma(SP, fl(Xm), zin[r0 - 1:r0 + 127, cols], [g.t_zin], [Xm.t], L + "1")
            if ti == NTL - 1:
                P.memset(DVE, fl(Xp), 0.0, [Xp.t])
                P.dma(SP, fl(Xp)[0:127, :], zin[r0 + 1:r0 + 128, cols], [g.t_zin], [Xp.t], L + "2")
            else:
                P.dma(SP, fl(Xp), zin[r0 + 1:r0 + 129, cols], [g.t_zin], [Xp.t], L + "2")
            P.tt(DVE, fl(yv), fl(Xm), wc.ap[:, 0, :], ALU.mult, [Xm.t, wc.t], [yv.t])
            yield
            P.tt(DVE, fl(tv), fl(X0), wc.ap[:, 1, :], ALU.mult, [X0.t, wc.t], [tv.t])
            yield
            P.tt(DVE, fl(yv), fl(yv), fl(tv), ALU.add, [yv.t, tv.t], [yv.t])
            yield
            P.tt(DVE, fl(tv), fl(Xp), wc.ap[:, 2, :], ALU.mult, [Xp.t, wc.t], [tv.t])
            yield
            P.tt(DVE, fl(yv), fl(yv), fl(tv), ALU.add, [yv.t, tv.t], [yv.t])
            yield
            P.act(yv.ap, yv.ap, AF.Silu, [yv.t], [yv.t])
            yield
            if gi == 2:
                P.tt(DVE, Y.ap[:, :, 0:128], yv.ap, bc_last(beta, 128), ALU.mult, [yv.t, gs.t], [Y.t])
            else:
                P.act(tv.ap, yv.ap, AF.Square, [yv.t], [tv.t])
                P.red(ssq.ap, tv.ap, ALU.add, [tv.t], [ssq.t])
                P.act(ssq.ap, ssq.ap, AF.Sqrt, [ssq.t], [ssq.t], bias=EPS)
                P.recip(ssq.ap, ssq.ap, [ssq.t], [ssq.t])
                yield
                if gi == 0:
                    P.stt(DVE, tv.ap, yv.ap, 128 ** -0.5, bc_last(ssq.ap, 128), ALU.mult, ALU.mult, [yv.t, ssq.t], [tv.t])
                    P.cp(ACT, qnb.ap, tv.ap, [tv.t], [qnb.t])
                    P.tt(POOL, qeb.ap, tv.ap, bc_last(eg, 128), ALU.mult, [tv.t, gs.t], [qeb.t])
                else:
                    P.tt(DVE, tv.ap, yv.ap, bc_last(ssq.ap, 128), ALU.mult, [yv.t, ssq.t], [tv.t])
                    P.cp(ACT, knb.ap, tv.ap, [tv.t], [knb.t])
                    P.tt(POOL, kdb.ap, tv.ap, bc_last(egl, 128), ALU.mult, [tv.t, gs.t], [kdb.t])
                    P.tt(POOL, Y.ap[:, :, 128:256], tv.ap, bc_last(beg, 128), ALU.mult, [tv.t, gs.t], [Y.t])
            yield
        for h in range(8):
            P.tr(trb(7)[:, h, :], knb.ap[:, h, :], k.identb.ap, [knb.t, k.identb.t], [bt[7]])
        P.cp(DVE, kT.ap, trb(7), [bt[7]], [kT.t])
        yield
        for h in range(8):
            P.tr(trb(0)[:, h, :], qnb.ap[:, h, :], k.identb.ap, [qnb.t, k.identb.t], [bt[0]])
        P.cp(ACT, qT.ap, trb(0), [bt[0]], [qT.t])
        yield
        for h in range(8):
            P.tr(trb(7)[:, h, :], qeb.ap[:, h, :], k.identb.ap, [qeb.t, k.identb.t], [bt[7]])
        P.cp(DVE, qeT.ap, trb(7), [bt[7]], [qeT.t])
        yield
        for h in range(8):
            P.mm(ng_v[:, h, :], kT.ap[:, h, :], kT.ap[:, h, :], True, True, [kT.t], [bt[1], bt[2]])
        P.tt(DVE, Nf.ap, ng_v, Ls.ap, ALU.mult, [bt[1], bt[2], Ls.t], [Nf.t])
        yield
        P.tt(POOL, Nf.ap, Nf.ap, bc_last(nbeta, 128), ALU.mult, [Nf.t, gs.t], [Nf.t])
        for h in range(8):
            P.tr(ng_v[:, h, :], Nf.ap[:, h, :], k.identf.ap, [Nf.t, k.identf.t], [bt[1], bt[2]])
        P.cp(ACT, NTf.ap, ng_v, [bt[1], bt[2]], [NTf.t])
        yield
        for half in range(2):
            hs = slice(half * 4, half * 4 + 4)
            P.tt(POOL, sb["P"].ap, Nf.ap[:, hs, :], bc_mid(bm.ap[:, 0, :], 4), ALU.mult, [Nf.t, bm.t], [sb["P"].t])
            P.tt(DVE, sb["PT"].ap, NTf.ap[:, hs, :], bc_mid(bm.ap[:, 0, :], 4), ALU.mult, [NTf.t, bm.t], [sb["PT"].t])
            P.tt(DVE, sb["T"].ap, sb["P"].ap, bc_mid(k.identf.ap, 4), ALU.add, [sb["P"].t, k.identf.t], [sb["T"].t])
            P.tt(POOL, sb["TT"].ap, sb["PT"].ap, bc_mid(k.identf.ap, 4), ALU.add, [sb["PT"].t, k.identf.t], [sb["TT"].t])
            yield
            for lev in range(4):
                for hh in range(4):
                    P.mm(v4(3)[:, hh, :], sb["PT"].ap[:, hh, :], sb["P"].ap[:, hh, :], True, True, [sb["PT"].t, sb["P"].t], [bt[3]])
                for hh in range(4):
                    P.mm(v4(4)[:, hh, :], sb["P"].ap[:, hh, :], sb["PT"].ap[:, hh, :], True, True, [sb["PT"].t, sb["P"].t], [bt[4]])
                P.cp(ACT, sb["P"].ap, v4(3), [bt[3]], [sb["P"].t])
                P.cp(ACT, sb["PT"].ap, v4(4), [bt[4]], [sb["PT"].t])
                yield
                for hh in range(4):
                    P.mm(v4(5)[:, hh, :], sb["TT"].ap[:, hh, :], sb["P"].ap[:, hh, :], True, True, [sb["TT"].t, sb["P"].t], [bt[5]])
                for hh in range(4):
                    P.mm(v4(6)[:, hh, :], sb["P"].ap[:, hh, :], sb["TT"].ap[:, hh, :], True, True, [sb["TT"].t, sb["P"].t], [bt[6]])
                P.tt(DVE, sb["T"].ap, sb["T"].ap, v4(5), ALU.add, [sb["T"].t, bt[5]], [sb["T"].t])
                P.tt(DVE, sb["TT"].ap, sb["TT"].ap, v4(6), ALU.add, [sb["TT"].t, bt[6]], [sb["TT"].t])
                yield
            for mi in (1, 2):
                P.tt(POOL, sb["No"].ap, Nf.ap[:, hs, :], bc_mid(bm.ap[:, mi, :], 4), ALU.mult, [Nf.t, bm.t], [sb["No"].t])
                P.tt(DVE, sb["NoT"].ap, NTf.ap[:, hs, :], bc_mid(bm.ap[:, mi, :], 4), ALU.mult, [NTf.t, bm.t], [sb["NoT"].t])
                for hh in range(4):
                    P.mm(v4(3)[:, hh, :], sb["NoT"].ap[:, hh, :], sb["T"].ap[:, hh, :], True, True, [sb["NoT"].t, sb["T"].t], [bt[3]])
                for hh in range(4):
                    P.mm(v4(4)[:, hh, :], sb["No"].ap[:, hh, :], sb["TT"].ap[:, hh, :], True, True, [sb["No"].t, sb["TT"].t], [bt[4]])
                P.cp(ACT, sb["A"].ap, v4(3), [bt[3]], [sb["A"].t])
                P.cp(ACT, sb["B"].ap, v4(4), [bt[4]], [sb["B"].t])
                yield
                for hh in range(4):
                    P.mm(v4(5)[:, hh, :], sb["TT"].ap[:, hh, :], sb["A"].ap[:, hh, :], True, True, [sb["TT"].t, sb["A"].t], [bt[5]])
                for hh in range(4):
                    P.mm(v4(6)[:, hh, :], sb["T"].ap[:, hh, :], sb["B"].ap[:, hh, :], True, True, [sb["T"].t, sb["B"].t], [bt[6]])
                P.tt(DVE, sb["T"].ap, sb["T"].ap, v4(5), ALU.add, [sb["T"].t, bt[5]], [sb["T"].t])
                P.tt(DVE, sb["TT"].ap, sb["TT"].ap, v4(6), ALU.add, [sb["TT"].t, bt[6]], [sb["TT"].t])
                yield
            yv_ = bank(k, 3, 2).rearrange("p (h c) -> p h c", h=4)
            for hh in range(4):
                h = half * 4 + hh
                P.mm(yv_[:, hh, :], sb["TT"].ap[:, hh, :], Y.ap[:, h, :], True, True, [sb["TT"].t, Y.t], [bt[3], bt[4]])
            P.cp(ACT if half == 0 else DVE, Y.ap[:, hs, :], yv_, [bt[3], bt[4]], [Y.t])
            yield
        P.ts(DVE, nkc.ap, Y.ap[:, :, 128:256], -1.0, None, ALU.mult, None, [Y.t], [nkc.t])
        for h in range(8):
            P.tr(trb(7)[:, h, :], nkc.ap[:, h, :], k.identb.ap, [nkc.t, k.identb.t], [bt[7]])
        P.cp(ACT, nkcT.ap, trb(7), [bt[7]], [nkcT.t])
        for h in range(8):
            P.mm(ng_v[:, h, :], nkcT.ap[:, h, :], Sb.ap[:, h, :], True, True, [nkcT.t, Sb.t], [bt[1], bt[2]])
        P.tt(DVE, ub.ap, Y.ap[:, :, 0:128], ng_v, ALU.add, [Y.t, bt[1], bt[2]], [ub.t])
        yield
        for h in range(8):
            P.mm(ng_v[:, h, :], kT.ap[:, h, :], qT.ap[:, h, :], True, True, [kT.t, qT.t], [bt[1], bt[2]])
        P.tt(DVE, attnT.ap, ng_v, LT.ap, ALU.mult, [bt[1], bt[2], LT.t], [attnT.t])
        yield
        for h in range(8):
            P.mm(ng_v[:, h, :], qeT.ap[:, h, :], Sb.ap[:, h, :], True, False, [qeT.t, Sb.t], [bt[1], bt[2]])
            P.mm(ng_v[:, h, :], attnT.ap[:, h, :], ub.ap[:, h, :], False, True, [attnT.t, ub.t], [bt[1], bt[2]])
        P.cp(ACT, ot.ap, ng_v, [bt[1], bt[2]], [ot.t])
        P.dma(SP, o_dst[rows, :], fl(ot), [ot.t], [t_o], L + "o")
        yield
        for h in range(8):
            P.mm(ng_v[:, h, :], kdb.ap[:, h, :], ub.ap[:, h, :], True, True, [kdb.t, ub.t], [bt[1], bt[2]])
        P.tt(DVE, S.ap, tmpS.ap, ng_v, ALU.add, [tmpS.t, bt[1], bt[2]], [S.t])
        P.cp(ACT, Sb.ap, S.ap, [S.t], [Sb.t])
        yield
    if not is_s:
        P.dma(SP, k.o["nsd"][si, dr].rearrange("h d v -> d h v"), S.ap, [S.t], [], L + "so")


def na_pattern_j(p):
    return [0, 1, 2, 14, 15][p]


def na_kb(j):
    return min(max(2 * j - 4, 0), 23)


def na_pat(j):
    if j <= 1:
        return j
    if j <= 13:
        return 2
    return j - 11


_CONST = {}


def host_consts():
    if _CONST:
        return _CONST
    idx = np.arange(128)
    s, c = idx[:, None], idx[None, :]
    bd = lambda b: (s // b) == (c // b)
    tri = np.stack([s <= c, s > c, s >= c, s < c, s <= c, s >= c, s < c, s > c,
                    bd(32), bd(64) & ~bd(32), ~bd(64)]).astype(np.float32)
    t = np.arange(NS_TOK)
    inv = 1.0 / (10000.0 ** (np.arange(32, dtype=np.float32) / 32))
    pos = np.stack([t // 64, t % 64], axis=1).astype(np.float32)
    ang = pos[:, :, None] * inv
    rope = np.concatenate([np.cos(ang).reshape(NS_TOK, 64), np.sin(ang).reshape(NS_TOK, 64)], axis=1).astype(np.float32)
    gi = np.zeros((5, 128, 576, 2), np.int64)
    inwin = np.zeros((5, 128, 576), bool)
    for p in range(5):
        j = na_pattern_j(p)
        kb = na_kb(j)
        qi = np.arange(128)
        r = 2 * j + qi // 64
        cq = qi % 64
        ki = np.arange(576)
        kr = kb + ki // 64
        kc = ki % 64
        rs = np.clip(r - 4, 0, 24)
        cs = np.clip(cq - 8, 0, 48)
        okr = (kr[None, :] >= rs[:, None]) & (kr[None, :] < rs[:, None] + 8)
        okc = (kc[None, :] >= cs[:, None]) & (kc[None, :] < cs[:, None] + 16)
        ok = okr & okc
        inwin[p] = ok
        gi[p, :, :, 0] = np.where(ok, kr[None, :] - r[:, None] + 7, 0)
        gi[p, :, :, 1] = np.where(ok, kc[None, :] - cq[:, None] + 15, 0)
    _CONST.update(ident=np.eye(128, dtype=np.float32), tri=tri, rope=rope, gi=gi, inwin=inwin,
                  na_mask=np.where(inwin, 0.0, -30000.0).astype(np.float32))
    return _CONST


def prep_shared(inp):
    C = host_consts()
    f = lambda a: np.ascontiguousarray(a, dtype=np.float32)
    rpb = inp["od_rpb"][0]
    gath = rpb[:, C["gi"][..., 0], C["gi"][..., 1]]
    gath = np.where(C["inwin"][None], gath, np.float32(0))
    na_bias = f(np.transpose(gath, (1, 2, 0, 3)).reshape(5, 128, 8 * 576))
    sh = {
        "w_ada": f(inp["w_ada"]), "b_ada": f(inp["b_ada"]), "norm1": f(inp["norm1"]), "norm2": f(inp["norm2"]),
        "w_mlp1": f(inp["w_mlp1"]), "w_mlp2": f(inp["w_mlp2"]), "ev_w_in": f(inp["ev_w_in"][0]),
        "wa2": f(np.concatenate([inp["ev_w_a2"][0], inp["ev_b_a2"][0][:, None, :]], axis=1)),
        "gla_norm": f(inp["ev_gla_norm"][0:1]),
        "qk_gain": f(np.concatenate([np.tile(inp["ev_q_norm"][0], 8), np.tile(inp["ev_k_norm"][0], 2)])[None, :]),
        "ev_w_out": f(inp["ev_w_out"][0]), "od_w_in": f(inp["od_w_in"][0]), "od_conv": f(inp["od_conv"][0]),
        "od_alog": f(inp["od_a_log"][0].reshape(1, 16)), "od_dtb": f(inp["od_dt_bias"][0].reshape(1, 16)),
        "dn_norm": f(inp["od_dn_norm"][0:1]), "na_bias": na_bias, "na_mask": C["na_mask"],
        "od_w_out": f(inp["od_w_out"][0]), "norm_f": f(inp["norm_f"][None, :]),
        "ident": C["ident"], "tri": C["tri"], "rope": C["rope"],
    }
    return sh


def prep_core(inp, sh, i):
    f = lambda a: np.ascontiguousarray(a, dtype=np.float32)
    s = i // 2
    cond = np.stack([inp["c_ctx"], inp["c"][s]], axis=0)
    condT = f(cond.reshape(2, 16, 128).transpose(2, 1, 0).reshape(128, 32))
    m = dict(sh)
    m.update({
        "xp": f(inp["x_prompt"][2 * i:2 * i + 2].reshape(NP_TOK, D)), "xs": f(inp["x_sample"][s]), "condT": condT,
        "sgla": f(inp["state_gla"][s, 0]), "cgk": f(inp["cache_gqa_k"][s, 0].reshape(PAST, 256)),
        "cgv": f(inp["cache_gqa_v"][s, 0].reshape(PAST, 256)), "cnk": f(inp["cache_na_k"][s, 0].reshape(PAST, 1024)),
        "cnv": f(inp["cache_na_v"][s, 0].reshape(PAST, 1024)), "sdl": f(inp["state_delta"][s, 0]),
    })
    return m


_PROG = {}


def kernel(**inputs):
    inp = {k_: np.asarray(v) for k_, v in inputs.items()}
    if "nc" not in _PROG:
        _PROG["nc"] = build_program()
    nc = _PROG["nc"]
    sh = prep_shared(inp)
    in_maps = [prep_core(inp, sh, i) for i in range(NCORES)]
    res = run_bass_kernel_spmd(nc, in_maps, core_ids=list(range(NCORES))).results
    B = 16
    y_prompt = np.concatenate([r["yp"].reshape(2, 256, D) for r in res], axis=0)
    y_sample = np.stack([res[2 * s]["ys"] for s in range(4)], axis=0)
    st_gla = np.concatenate([r["nsg"].reshape(2, 1, 2, 8, 64, 128) for r in res], axis=0)
    ck_gqa = np.concatenate([r["ngk"].reshape(2, 1, 256, 2, 128) for r in res], axis=0)
    cv_gqa = np.concatenate([r["ngv"].reshape(2, 1, 256, 2, 128) for r in res], axis=0)
    ck_na = np.concatenate([r["nnk"].reshape(2, 1, 256, 8, 128) for r in res], axis=0)
    cv_na = np.concatenate([r["nnv"].reshape(2, 1, 256, 8, 128) for r in res], axis=0)
    st_dn = np.concatenate([r["nsd"].reshape(2, 1, 2, 8, 128, 128) for r in res], axis=0)
    outs = (y_prompt, y_sample, st_gla, ck_gqa, cv_gqa, ck_na, cv_na, st_dn)
    return tuple(np.ascontiguousarray(o, dtype=np.float32) for o in outs)
```
